# Optimizing a Trainium2 kernel written in Bass

```python
import jax, jax.numpy as jnp
from jax import lax
import numpy as np

D_MODEL = 1024
BATCH = 16
SEQ = 2048
DEPTH = 2

HEAD_DIM = 64
SB_HEADS = 8
SB_WIDTH = SB_HEADS * HEAD_DIM
CONV_CH = 512
CONV_WIDTH = 31
NSA_HEADS = 8
NSA_KV_HEADS = 2
NSA_GROUP = NSA_HEADS // NSA_KV_HEADS
NSA_WIDTH = NSA_HEADS * HEAD_DIM
KV_WIDTH = NSA_KV_HEADS * HEAD_DIM
CMP_BLOCK = 32
CMP_STRIDE = 16
SEL_BLOCK = 64
SEL_TOPK = 8
WINDOW = 512
GMLP_WIDTH = 512
GMLP_GROUPS = 4
GMLP_CHUNK = 128
D_FF = -(-8 * D_MODEL // (3 * 256)) * 256
Q_BLOCK = 128
SEL_Q_BLOCK = 64
ALPHA = (2 * DEPTH) ** 0.25
OUT_INIT = (8 * DEPTH) ** -0.25
LN_EPS = 1e-5
NEG = -1e30
FORCE_BONUS = 1e3

EVEN_SPLITS = (SB_WIDTH, SB_WIDTH, SB_WIDTH, CONV_CH, CONV_CH)
ODD_SPLITS = (NSA_WIDTH,) + (KV_WIDTH,) * 6 + (3 * NSA_HEADS, GMLP_WIDTH, GMLP_WIDTH)
EVEN_IN = sum(EVEN_SPLITS)
ODD_IN = sum(ODD_SPLITS)
MIX_OUT = SB_WIDTH + CONV_CH

kernel_name = "hybrid_stickbreak_conformer_nsa_gmlp_deepnorm"


def layer_norm(x, g, b):
    xf = x.astype(jnp.float32)
    mu = jnp.mean(xf, axis=-1, keepdims=True)
    var = jnp.mean(jnp.square(xf - mu), axis=-1, keepdims=True)
    y = (xf - mu) * lax.rsqrt(var + LN_EPS) * g.astype(jnp.float32) + b.astype(jnp.float32)
    return y.astype(x.dtype)


def split_cols(h, sizes):
    offs = [int(o) for o in np.cumsum(sizes)[:-1]]
    return jnp.split(h, offs, axis=-1)


def masked_softmax(s, mask):
    s = jnp.where(mask, s, NEG)
    p = jax.nn.softmax(s, axis=-1)
    return jnp.where(mask, p, 0.0)


def stick_breaking_attention(q, k, v):
    B, S, H, D = q.shape
    scale = D ** -0.5
    outs = []
    for i in range(S // Q_BLOCK):
        q0 = i * Q_BLOCK
        q1 = q0 + Q_BLOCK
        z = jnp.einsum('bqhd,bkhd->bhqk', q[:, q0:q1], k[:, :q1]).astype(jnp.float32) * scale
        tpos = q0 + jnp.arange(Q_BLOCK)
        kpos = jnp.arange(q1)
        past = kpos[None, :] < tpos[:, None]
        log_beta = jax.nn.log_sigmoid(z)
        log_rem = jnp.where(past, log_beta - z, 0.0)
        between = lax.cumsum(log_rem, axis=3, reverse=True) - log_rem
        w = jnp.where(past, jnp.exp(log_beta + between), 0.0)
        outs.append(jnp.einsum('bhqk,bkhd->bqhd', w.astype(v.dtype), v[:, :q1]))
    return jnp.concatenate(outs, axis=1)


def conformer_conv(a, gate, w_dw, b_dw, ln_g, ln_b):
    C = a.shape[-1]
    h = a * jax.nn.sigmoid(gate)
    hp = jnp.pad(h, ((0, 0), (CONV_WIDTH - 1, 0), (0, 0)))
    y = lax.conv_general_dilated(hp, w_dw[:, None, :].astype(h.dtype), window_strides=(1,),
                                 padding='VALID', dimension_numbers=('NWC', 'WIO', 'NWC'),
                                 feature_group_count=C) + b_dw
    y = layer_norm(y, ln_g, ln_b)
    return jax.nn.silu(y)


def compress_blocks(blocks, pos, w1, w2):
    B, n, L, G, D = blocks.shape
    h = blocks + pos[:, None, :]
    h = jnp.transpose(h, (0, 1, 3, 2, 4)).reshape(B, n, G, L * D)
    return jax.nn.gelu(h @ w1) @ w2


def nsa_compressed(q5, k, v, pos_k, w1k, w2k, pos_v, w1v, w2v):
    B, S, G, R, D = q5.shape
    scale = D ** -0.5
    n_cmp = (S - CMP_BLOCK) // CMP_STRIDE + 1
    idx = jnp.arange(n_cmp)[:, None] * CMP_STRIDE + jnp.arange(CMP_BLOCK)[None, :]
    kc = compress_blocks(k[:, idx], pos_k, w1k, w2k)
    vc = compress_blocks(v[:, idx], pos_v, w1v, w2v)
    s = jnp.einsum('bsgrd,bngd->bgrsn', q5, kc).astype(jnp.float32) * scale
    tpos = jnp.arange(S)
    cstart = jnp.arange(n_cmp) * CMP_STRIDE
    mask = (cstart + CMP_BLOCK - 1)[None, :] <= tpos[:, None]
    p = masked_softmax(s, mask)
    o = jnp.einsum('bgrsn,bngd->bsgrd', p.astype(vc.dtype), vc)
    n_slc = S // SEL_BLOCK
    sstart = jnp.arange(n_slc) * SEL_BLOCK
    overlap = ((cstart[:, None] < sstart[None, :] + SEL_BLOCK) &
               (cstart[:, None] + CMP_BLOCK > sstart[None, :])).astype(jnp.float32)
    imp = jnp.einsum('bgrsn,nj->bsgj', p, overlap)
    return o, imp


def select_blocks(imp):
    B, S, G, n_slc = imp.shape
    blk_t = jnp.arange(S) // SEL_BLOCK
    j = jnp.arange(n_slc)
    valid = j[None, :] <= blk_t[:, None]
    forced = (j[None, :] == 0) | (j[None, :] == blk_t[:, None]) | (j[None, :] == blk_t[:, None] - 1)
    bonus = jnp.where(forced, FORCE_BONUS, 0.0)
    score = jnp.where(valid[None, :, None, :], imp + bonus[None, :, None, :], NEG)
    k_eff = min(SEL_TOPK, n_slc)
    _, sel = lax.top_k(score, k_eff)
    return sel


def nsa_selected(q5, k, v, sel):
    B, S, G, R, D = q5.shape
    scale = D ** -0.5
    n_slc = S // SEL_BLOCK
    kk = sel.shape[-1]
    kb = jnp.transpose(k.reshape(B, n_slc, SEL_BLOCK, G, D), (0, 3, 1, 2, 4))
    vb = jnp.transpose(v.reshape(B, n_slc, SEL_BLOCK, G, D), (0, 3, 1, 2, 4))
    nc = S // SEL_Q_BLOCK
    qc = jnp.moveaxis(q5.reshape(B, nc, SEL_Q_BLOCK, G, R, D), 1, 0)
    ic = jnp.moveaxis(sel.reshape(B, nc, SEL_Q_BLOCK, G, kk), 1, 0)
    starts = jnp.arange(nc, dtype=jnp.int32) * SEL_Q_BLOCK
    bi = jnp.arange(B)[:, None, None, None]
    gi = jnp.arange(G)[None, None, :, None]

    def one(args):
        qi, idx, st = args
        kg = kb[bi, gi, idx]
        vg = vb[bi, gi, idx]
        tpos = st + jnp.arange(SEL_Q_BLOCK)
        kpos = idx[..., None] * SEL_BLOCK + jnp.arange(SEL_BLOCK)
        mask = (kpos <= tpos[None, :, None, None, None]).reshape(B, SEL_Q_BLOCK, G, kk * SEL_BLOCK)
        s = jnp.einsum('bqgrd,bqgkld->bqgrkl', qi, kg).astype(jnp.float32) * scale
        s = s.reshape(B, SEL_Q_BLOCK, G, R, kk * SEL_BLOCK)
        p = masked_softmax(s, mask[:, :, :, None, :])
        return jnp.einsum('bqgrm,bqgmd->bqgrd', p.astype(vg.dtype),
                          vg.reshape(B, SEL_Q_BLOCK, G, kk * SEL_BLOCK, D))

    o = lax.map(one, (qc, ic, starts))
    return jnp.moveaxis(o, 0, 1).reshape(B, S, G, R, D)


def nsa_window(q5, k, v):
    B, S, G, R, D = q5.shape
    scale = D ** -0.5
    nq = S // Q_BLOCK
    span = Q_BLOCK + WINDOW
    kp = jnp.pad(k, ((0, 0), (WINDOW, 0), (0, 0), (0, 0)))
    vp = jnp.pad(v, ((0, 0), (WINDOW, 0), (0, 0), (0, 0)))
    qb = jnp.moveaxis(q5.reshape(B, nq, Q_BLOCK, G, R, D), 1, 0)
    starts = jnp.arange(nq, dtype=jnp.int32) * Q_BLOCK

    def one(args):
        qi, st = args
        kb = lax.dynamic_slice_in_dim(kp, st, span, axis=1)
        vb = lax.dynamic_slice_in_dim(vp, st, span, axis=1)
        tpos = st + jnp.arange(Q_BLOCK)
        kpos = st - WINDOW + jnp.arange(span)
        mask = ((kpos[None, :] >= 0) & (kpos[None, :] <= tpos[:, None]) &
                (tpos[:, None] - kpos[None, :] < WINDOW))
        s = jnp.einsum('bqgrd,bkgd->bgrqk', qi, kb).astype(jnp.float32) * scale
        p = masked_softmax(s, mask)
        return jnp.einsum('bgrqk,bkgd->bqgrd', p.astype(vb.dtype), vb)

    o = lax.map(one, (qb, starts))
    return jnp.moveaxis(o, 0, 1).reshape(B, S, G, R, D)


def chunked_gmlp(u, v, ln_g, ln_b, ws, bs):
    u = jax.nn.gelu(u)
    v = layer_norm(jax.nn.gelu(v), ln_g, ln_b)
    B, S, C = v.shape
    nch = S // GMLP_CHUNK
    cg = C // GMLP_GROUPS
    vr = v.reshape(B, nch, GMLP_CHUNK, GMLP_GROUPS, cg)
    tril = jnp.tril(jnp.ones((GMLP_CHUNK, GMLP_CHUNK), ws.dtype))
    mixed = jnp.einsum('gts,bcsgd->bctgd', ws * tril, vr) + jnp.transpose(bs)[None, None, :, :, None]
    return u * mixed.reshape(B, S, C)


def even_mixer(x, w_in, conv_w, conv_b, conv_ln_g, conv_ln_b, w_out):
    B, S, _ = x.shape
    q, k, v, a, g = split_cols(x @ w_in, EVEN_SPLITS)
    hs = (B, S, SB_HEADS, HEAD_DIM)
    o_sb = stick_breaking_attention(q.reshape(hs), k.reshape(hs), v.reshape(hs)).reshape(B, S, SB_WIDTH)
    o_cv = conformer_conv(a, g, conv_w, conv_b, conv_ln_g, conv_ln_b)
    return jnp.concatenate([o_sb, o_cv], axis=-1) @ w_out


def odd_mixer(x, w_in, cmpk_pos, cmpk_w1, cmpk_w2, cmpv_pos, cmpv_w1, cmpv_w2,
              gmlp_ln_g, gmlp_ln_b, gmlp_ws, gmlp_bs, w_out):
    B, S, _ = x.shape
    q, kc, vc, ks, vs, kw, vw, gt, u, v = split_cols(x @ w_in, ODD_SPLITS)
    q5 = q.reshape(B, S, NSA_KV_HEADS, NSA_GROUP, HEAD_DIM)
    kvs = (B, S, NSA_KV_HEADS, HEAD_DIM)
    o_cmp, imp = nsa_compressed(q5, kc.reshape(kvs), vc.reshape(kvs),
                                cmpk_pos, cmpk_w1, cmpk_w2, cmpv_pos, cmpv_w1, cmpv_w2)
    sel = select_blocks(imp)
    o_slc = nsa_selected(q5, ks.reshape(kvs), vs.reshape(kvs), sel)
    o_win = nsa_window(q5, kw.reshape(kvs), vw.reshape(kvs))
    gates = jax.nn.sigmoid(gt).reshape(B, S, NSA_KV_HEADS, NSA_GROUP, 3)
    o_nsa = (gates[..., 0:1] * o_cmp + gates[..., 1:2] * o_slc + gates[..., 2:3] * o_win).reshape(B, S, NSA_WIDTH)
    o_mlp = chunked_gmlp(u, v, gmlp_ln_g, gmlp_ln_b, gmlp_ws, gmlp_bs)
    return jnp.concatenate([o_nsa, o_mlp], axis=-1) @ w_out


def swiglu(x, w_gate, w_up, w_down):
    return (jax.nn.silu(x @ w_gate) * (x @ w_up)) @ w_down


def setup_inputs(seed: int = 0) -> dict:
    key = jax.random.key(seed)
    ks = jax.random.split(key, 32)
    ne = (DEPTH + 1) // 2
    no = DEPTH // 2

    def nrm(k, shape, scale):
        return jax.random.normal(k, shape, jnp.float32) * scale

    L = CMP_BLOCK
    return {
        "x": nrm(ks[0], (BATCH, SEQ, D_MODEL), 1.0),
        "ev_w_in": nrm(ks[1], (ne, D_MODEL, EVEN_IN), D_MODEL ** -0.5),
        "ev_conv_w": nrm(ks[2], (ne, CONV_WIDTH, CONV_CH), CONV_WIDTH ** -0.5),
        "ev_conv_b": nrm(ks[3], (ne, CONV_CH), 0.02),
        "ev_conv_ln_g": 1.0 + nrm(ks[4], (ne, CONV_CH), 0.02),
        "ev_conv_ln_b": nrm(ks[5], (ne, CONV_CH), 0.02),
        "ev_w_out": nrm(ks[6], (ne, MIX_OUT, D_MODEL), MIX_OUT ** -0.5 * OUT_INIT),
        "od_w_in": nrm(ks[7], (no, D_MODEL, ODD_IN), D_MODEL ** -0.5),
        "od_cmpk_pos": nrm(ks[8], (no, L, HEAD_DIM), 0.02),
        "od_cmpk_w1": nrm(ks[9], (no, L * HEAD_DIM, HEAD_DIM), (L * HEAD_DIM) ** -0.5),
        "od_cmpk_w2": nrm(ks[10], (no, HEAD_DIM, HEAD_DIM), HEAD_DIM ** -0.5),
        "od_cmpv_pos": nrm(ks[11], (no, L, HEAD_DIM), 0.02),
        "od_cmpv_w1": nrm(ks[12], (no, L * HEAD_DIM, HEAD_DIM), (L * HEAD_DIM) ** -0.5),
        "od_cmpv_w2": nrm(ks[13], (no, HEAD_DIM, HEAD_DIM), HEAD_DIM ** -0.5),
        "od_gmlp_ln_g": 1.0 + nrm(ks[14], (no, GMLP_WIDTH), 0.02),
        "od_gmlp_ln_b": nrm(ks[15], (no, GMLP_WIDTH), 0.02),
        "od_gmlp_ws": nrm(ks[16], (no, GMLP_GROUPS, GMLP_CHUNK, GMLP_CHUNK), GMLP_CHUNK ** -0.5),
        "od_gmlp_bs": 1.0 + nrm(ks[17], (no, GMLP_GROUPS, GMLP_CHUNK), 0.1),
        "od_w_out": nrm(ks[18], (no, MIX_OUT, D_MODEL), MIX_OUT ** -0.5 * OUT_INIT),
        "ffn_w_gate": nrm(ks[19], (DEPTH, D_MODEL, D_FF), D_MODEL ** -0.5),
        "ffn_w_up": nrm(ks[20], (DEPTH, D_MODEL, D_FF), D_MODEL ** -0.5),
        "ffn_w_down": nrm(ks[21], (DEPTH, D_FF, D_MODEL), D_FF ** -0.5 * OUT_INIT),
        "ln1_g": 1.0 + nrm(ks[22], (DEPTH, D_MODEL), 0.02),
        "ln1_b": nrm(ks[23], (DEPTH, D_MODEL), 0.02),
        "ln2_g": 1.0 + nrm(ks[24], (DEPTH, D_MODEL), 0.02),
        "ln2_b": nrm(ks[25], (DEPTH, D_MODEL), 0.02),
    }


def reference(x, ev_w_in, ev_conv_w, ev_conv_b, ev_conv_ln_g, ev_conv_ln_b, ev_w_out,
              od_w_in, od_cmpk_pos, od_cmpk_w1, od_cmpk_w2, od_cmpv_pos, od_cmpv_w1, od_cmpv_w2,
              od_gmlp_ln_g, od_gmlp_ln_b, od_gmlp_ws, od_gmlp_bs, od_w_out,
              ffn_w_gate, ffn_w_up, ffn_w_down, ln1_g, ln1_b, ln2_g, ln2_b):
    for layer in range(DEPTH):
        i = layer // 2
        if layer % 2 == 0:
            m = even_mixer(x, ev_w_in[i], ev_conv_w[i], ev_conv_b[i], ev_conv_ln_g[i],
                           ev_conv_ln_b[i], ev_w_out[i])
        else:
            m = odd_mixer(x, od_w_in[i], od_cmpk_pos[i], od_cmpk_w1[i], od_cmpk_w2[i],
                          od_cmpv_pos[i], od_cmpv_w1[i], od_cmpv_w2[i], od_gmlp_ln_g[i],
                          od_gmlp_ln_b[i], od_gmlp_ws[i], od_gmlp_bs[i], od_w_out[i])
        x = layer_norm(ALPHA * x + m, ln1_g[layer], ln1_b[layer])
        x = layer_norm(ALPHA * x + swiglu(x, ffn_w_gate[layer], ffn_w_up[layer], ffn_w_down[layer]),
                       ln2_g[layer], ln2_b[layer])
    return x
```

```python
import numpy as np
from contextlib import ExitStack
import concourse.bass as bass
import concourse.mybir as mybir
from concourse.bass_utils import run_bass_kernel_spmd

F32 = mybir.dt.float32
BF16 = mybir.dt.bfloat16
AF = mybir.ActivationFunctionType
ALU = mybir.AluOpType

SEM_LIMIT = 30000
NDMA_SEM = 12
NCORES = 8
S = 2048
D = 1024
DFF = 2816
NFC = 22
ALPHA = 4 ** 0.25
LN_EPS = 1e-5
EV_IN = 2560
OD_IN = 2328


class Buf:
    __slots__ = ("last_w", "readers")

    def __init__(self):
        self.last_w = None
        self.readers = []


class Op:
    __slots__ = ("eng", "fn", "deps", "odeps", "is_dma", "sig", "signal", "cost", "idx", "pri")

    def __init__(self, eng, fn, is_dma, cost=300.0):
        self.eng = eng
        self.fn = fn
        self.is_dma = is_dma
        self.deps = []
        self.odeps = []
        self.sig = None
        self.signal = False
        self.cost = cost
        self.idx = 0
        self.pri = 0


class Prog:
    ENGS = ("pe", "act", "dve", "pool", "sp")

    def __init__(self, nc):
        self.nc = nc
        self.es = ExitStack()
        self.pending = []
        self.known = {e: {} for e in self.ENGS}
        self.esem = {}
        self.ecnt = {}
        self.nsem = 0
        for e in self.ENGS:
            self._new_esem(e)
        self.dsem = {}
        self.dcnt = {}
        for q, n in (("sp", 36), ("pool", 56)):
            self.dsem[q] = [self._sem(f"d_{q}_{i}") for i in range(n)]
            self.dcnt[q] = 0
        self.nops = 0
        self.cur_bias = 0
        self.dma_live = []
        self.last_op = {e: None for e in self.ENGS}

    def _sem(self, name):
        self.nsem += 1
        return self.es.enter_context(self.nc.semaphore(f"{name}_{self.nsem}"))

    def _new_esem(self, e):
        self.esem[e] = self._sem(f"e_{e}")
        self.ecnt[e] = 0

    def op(self, eng, fn, reads=(), writes=(), is_dma=False, cost=300.0):
        o = Op(eng, fn, is_dma, cost)
        deps = set()
        for b in reads:
            if b.last_w is not None:
                deps.add(b.last_w)
        for b in writes:
            if b.last_w is not None:
                deps.add(b.last_w)
            for r in b.readers:
                deps.add(r)
        for d in deps:
            o.odeps.append(d)
            if d.eng == "pe" and eng == "pe" and not d.is_dma and not is_dma:
                continue
            o.deps.append(d)
        for b in writes:
            b.last_w = o
            b.readers = []
        for b in reads:
            if b not in writes:
                b.readers.append(o)
        o.pri = self.cur_bias
        self.pending.append(o)
        self.nops += 1
        if is_dma:
            self.dma_live.append(o)
        return o

    def barrier(self):
        self.flush()
        lasts = [o for o in self.last_op.values() if o is not None]
        dmas = list(self.dma_live)
        self.dma_live = []
        for e in self.ENGS:
            o = Op(e, lambda eng: eng.nop(), False, 50.0)
            o.deps = [d for d in lasts if d.eng != e] + dmas
            o.odeps = list(o.deps)
            self.pending.append(o)
        self.flush()

    def dma(self, q, out, in_, reads=(), writes=(), slow=False):
        nbytes = float(np.prod(out.shape)) * 3.0
        return self.op(q, lambda e: e.dma_start(out=out, in_=in_), reads, writes, is_dma=True, cost=nbytes)

    def schedule(self, ops):
        n = len(ops)
        pri = [0] * n
        for i, o in enumerate(ops):
            o.idx = i
            pri[i] = i + o.pri
        inwin = set(ops)
        indeg = [0] * n
        succ = [[] for _ in range(n)]
        for o in ops:
            for d in o.odeps:
                if d in inwin:
                    indeg[o.idx] += 1
                    succ[d.idx].append(o.idx)
        ready_t = [0.0] * n
        ready = {e: [] for e in self.ENGS}
        for o in ops:
            if indeg[o.idx] == 0:
                ready[o.eng].append(o.idx)
        eng_free = {e: 0.0 for e in self.ENGS}
        dma_free = 0.0
        order = {e: [] for e in self.ENGS}
        done = 0
        SYNC = 200.0
        while done < n:
            best = None
            for e in self.ENGS:
                r = ready[e]
                if not r:
                    continue
                T = eng_free[e]
                cand = None
                for i in r:
                    rt = ready_t[i]
                    key = (0.0, pri[i]) if rt <= T else (rt, pri[i])
                    if cand is None or key < cand[0]:
                        cand = (key, i)
                i = cand[1]
                st = max(T, ready_t[i])
                if best is None or (st, pri[i]) < (best[0], pri[best[1]]):
                    best = (st, i, e)
            st, i, e = best
            o = ops[i]
            ready[e].remove(i)
            if o.is_dma:
                eng_free[e] = st + 60.0
                xfer = o.cost / 3.0 / 150.0
                dma_free = max(dma_free, st) + xfer
                fin = dma_free + 1800.0
            else:
                fin = st + o.cost
                eng_free[e] = fin
            order[e].append(o)
            done += 1
            for j in succ[i]:
                s2 = ops[j]
                lat = 0.0 if (s2.eng == e and e == "pe" and not o.is_dma and not s2.is_dma) else SYNC
                t = fin + lat
                if t > ready_t[j]:
                    ready_t[j] = t
                indeg[j] -= 1
                if indeg[j] == 0:
                    ready[s2.eng].append(j)
        return order

    def flush(self):
        ops = self.pending
        self.pending = []
        if not ops:
            return
        pend = set(ops)
        per = self.schedule(ops)
        for e in self.ENGS:
            comp = [o for o in per[e] if not o.is_dma]
            if comp:
                self.last_op[e] = comp[-1]
        for o in ops:
            for d in o.deps:
                if d in pend and not d.is_dma:
                    d.signal = True
        for e in self.ENGS:
            comp = [o for o in per[e] if not o.is_dma]
            if comp:
                comp[-1].signal = True
            i = 0
            n = len(comp)
            while i < n:
                j = i
                while not comp[j].signal:
                    j += 1
                if self.ecnt[e] >= SEM_LIMIT:
                    self._new_esem(e)
                self.ecnt[e] += 1
                sv = (self.esem[e], self.ecnt[e])
                for k in range(i, j + 1):
                    comp[k].sig = sv
                i = j + 1
        prewait = {}
        for o in [x for e in self.ENGS for x in per[e]]:
            if o.is_dma:
                q = o.eng
                n = self.dcnt[q]
                pool = self.dsem[q]
                slot = n % len(pool)
                rnd = n // len(pool)
                o.sig = (pool[slot], 16 * (rnd + 1))
                if rnd > 0:
                    prewait[o] = (pool[slot], 16 * rnd)
                self.dcnt[q] = n + 1
        known = self.known
        with self.nc.Block() as block:
            def run(e):
                lst = per[e]

                def body(engine):
                    kn = known[e]
                    for o in lst:
                        best = {}
                        ws = [d.sig for d in o.deps]
                        if o in prewait:
                            ws.append(prewait[o])
                        for (s, v) in ws:
                            if kn.get(s, 0) >= v:
                                continue
                            if best.get(s, 0) < v:
                                best[s] = v
                        for s, v in best.items():
                            engine.wait_ge(s, v)
                            kn[s] = v
                        ins = o.fn(engine)
                        if o.is_dma:
                            ins.then_inc(o.sig[0], 16)
                        elif o.signal:
                            ins.then_inc(o.sig[0], 1)
                return body
            if per["pe"]:
                block.tensor(run("pe"))
            if per["act"]:
                block.scalar(run("act"))
            if per["dve"]:
                block.vector(run("dve"))
            if per["pool"]:
                block.gpsimd(run("pool"))
            if per["sp"]:
                block.sync(run("sp"))
        for o in ops:
            o.fn = None
            o.deps = None
            o.odeps = None

    def finish(self, out_bufs):
        self.op("sp", lambda e: e.nop(), reads=list(out_bufs), writes=())
        self.flush()
        self.es.close()


class DT:
    def __init__(self, ap):
        self.ap = ap
        self.bufs = {}

    def b(self, key):
        if key not in self.bufs:
            self.bufs[key] = Buf()
        return self.bufs[key]

    def all(self, pred=None):
        return [v for k, v in self.bufs.items() if pred is None or pred(k)]


class Rot:
    def __init__(self, items):
        self.items = items
        self.i = 0

    def next(self):
        it = self.items[self.i % len(self.items)]
        self.i += 1
        return it


def host_consts():
    c = {}
    c["ident"] = np.eye(128, dtype=np.float32)
    j = np.arange(128)[:, None]
    s = np.arange(128)[None, :]
    c["negiu"] = np.where(j >= s, -1.0, 0.0).astype(np.float32)
    c["tril"] = np.where(s >= j, 1.0, 0.0).astype(np.float32)
    sp = np.arange(128)[:, None]
    t = np.arange(512)[None, :]
    sbm = np.zeros((128, 4, 512), np.float32)
    cmi = np.zeros((128, 4, 512), np.float32)
    for rel in range(4):
        sbm[:, rel, :] = (sp + rel * 128 < t)
        cmi[:, rel, :] = (sp + rel * 128 <= t)
    c["sbmask"] = sbm
    c["cmaski"] = cmi
    wm = np.zeros((128, 8, 512), np.float32)
    for rel in range(8):
        sa = (rel - 4) * 128 + sp
        wm[:, rel, :] = (sa <= t) & (t - sa < 512)
    c["wmask"] = wm
    n = np.arange(127)[:, None]
    tt = np.arange(S)[None, :]
    cm = np.zeros((128, S), np.float32)
    cm[:127] = (16 * n + 31 <= tt)
    c["cmask"] = cm
    cstart = np.arange(127) * 16
    sstart = np.arange(32) * 64
    ov = np.zeros((128, 32), np.float32)
    ov[:127] = ((cstart[:, None] < sstart[None, :] + 64) & (cstart[:, None] + 32 > sstart[None, :]))
    c["overlap"] = ov
    blk = np.arange(S) // 64
    jj = np.arange(32)[None, :]
    valid = jj <= blk[:, None]
    forced = (jj == 0) | (jj == blk[:, None]) | (jj == blk[:, None] - 1)
    sa = np.where(valid, np.where(forced, 1e3, 0.0), -1e30).astype(np.float32)
    c["scoreadd"] = np.ascontiguousarray(sa.reshape(16, 128, 32).transpose(1, 0, 2))
    E = np.zeros((32, 16, 128), np.float32)
    for kb in range(16):
        for ss in range(128):
            E[2 * kb + ss // 64, kb, ss] = 1.0
    c["emat"] = E
    NEGB = -30000.0
    en = np.zeros((128, 16, 128), np.float32)
    en[:32] = E * NEGB
    c["ematn"] = en
    c["cmneg"] = np.where(cm > 0, 0.0, NEGB).astype(np.float32)
    tri = np.zeros((128, 2, 128), np.float32)
    tri[:, 0, :] = np.where(j <= s, 0.0, NEGB)
    tri[:, 1, :] = np.where(j > s, 0.0, NEGB)
    c["trineg"] = tri
    return c


CONST_SHAPES = {
    "ident": [128, 128], "negiu": [128, 128], "tril": [128, 128], "sbmask": [128, 4, 512],
    "cmaski": [128, 4, 512], "wmask": [128, 8, 512], "cmask": [128, S], "overlap": [128, 32],
    "scoreadd": [128, 16, 32], "emat": [32, 16, 128], "ematn": [128, 16, 128], "cmneg": [128, S],
    "trineg": [128, 2, 128],
}

WEIGHT_SHAPES = {
    "ev_w_in": [D, EV_IN], "ev_conv_w": [31, 512], "ev_conv_b": [512], "ev_conv_ln_g": [512],
    "ev_conv_ln_b": [512], "ev_w_out": [D, D], "od_w_in": [D, OD_IN], "od_cmpk_pos": [32, 64],
    "od_cmpk_w1": [2048, 64], "od_cmpk_w2": [64, 64], "od_cmpv_pos": [32, 64], "od_cmpv_w1": [2048, 64],
    "od_cmpv_w2": [64, 64], "od_gmlp_ln_g": [512], "od_gmlp_ln_b": [512], "od_gmlp_ws": [4, 128, 128],
    "od_gmlp_bs": [4, 128], "od_w_out": [D, D], "ffn_w_gate": [2, D, DFF], "ffn_w_up": [2, D, DFF],
    "ffn_w_down": [2, DFF, D], "ln1_g": [2, D], "ln1_b": [2, D], "ln2_g": [2, D], "ln2_b": [2, D],
}


def build(ntok=4096, debug_outs=(), stages=None):
    nseq = ntok // S
    NT = ntok // 128
    nc = bass.Bass("TRN2", target_bir_lowering=False)
    P = Prog(nc)
    IN = {}
    IN["x"] = nc.dram_tensor("x", [ntok, D], F32, kind="ExternalInput").ap()
    for k, shp in WEIGHT_SHAPES.items():
        IN[k] = nc.dram_tensor(k, shp, F32, kind="ExternalInput").ap()
    for k, shp in CONST_SHAPES.items():
        IN[k] = nc.dram_tensor("c_" + k, shp, F32, kind="ExternalInput").ap()
    out_dt = DT(nc.dram_tensor("out", [ntok, D], F32, kind="ExternalOutput").ap())

    pooled = len(debug_outs) == 0
    POOL_ELEMS = 12613632 + 4096
    pool_state = {"L0": 0, "L1": 0}
    if pooled:
        pool_ap = nc.dram_tensor("scr_pool", [POOL_ELEMS], BF16).ap()

    def scratch(name, shape, dt, grp=None):
        if pooled and grp is not None:
            n = int(np.prod(shape))
            off = pool_state[grp]
            pool_state[grp] = off + n
            assert pool_state[grp] <= POOL_ELEMS
            letters = "abcd"[:len(shape)]
            pat = "(" + " ".join(letters) + ") -> " + " ".join(letters)
            kw = {letters[i]: shape[i] for i in range(len(shape))}
            return DT(pool_ap[off:off + n].rearrange(pat, **kw))
        kind = "ExternalOutput" if name in debug_outs else "Internal"
        return DT(nc.dram_tensor(name, shape, dt, kind=kind).ap())

    SC = {}
    SC["qT0"] = scratch("qT0", [nseq, 4, 128, S], BF16, "L0")
    SC["kT0"] = scratch("kT0", [nseq, 4, 128, S], BF16, "L0")
    SC["v0"] = scratch("v0", [ntok, 512], BF16, "L0")
    SC["hT0"] = scratch("hT0", [nseq, 4, 128, S + 30], BF16, "L0")
    SC["osbT"] = scratch("osbT", [nseq, 4, 128, S], BF16, "L0")
    SC["ocvT"] = scratch("ocvT", [nseq, 4, 128, S], BF16, "L0")
    SC["x1"] = scratch("x1", [ntok, D], F32)
    if pooled:
        SC["x2"] = out_dt
        SC["x3"] = SC["x1"]
    else:
        SC["x2"] = scratch("x2", [ntok, D], F32)
        SC["x3"] = scratch("x3", [ntok, D], F32)
    SC["q1T"] = scratch("q1T", [nseq, 4, 128, S], BF16, "L1")
    for nm in ("kcT", "vcT", "ksT", "kwT"):
        SC[nm] = scratch(nm, [nseq, 128, S], BF16, "L1")
    SC["vsA"] = scratch("vsA", [ntok, 130], BF16, "L1")
    SC["vwA"] = scratch("vwA", [ntok, 130], BF16, "L1")
    SC["gates"] = scratch("gates", [ntok, 24], F32)
    SC["omlpT"] = scratch("omlpT", [nseq, 4, 128, S], BF16, "L1")
    SC["onsaT"] = scratch("onsaT", [nseq, 4, 128, S], BF16, "L1")

    top = ExitStack()

    _cnt = [0]

    def alloc(es, name, shape, dt=F32):
        _cnt[0] += 1
        return es.enter_context(nc.sbuf_tensor(f"sb{_cnt[0]}_{name}", shape, dt))

    PSB = []
    PSBIG = []
    for i in range(4):
        t = top.enter_context(nc.psum_tensor(f"psbig{i}", [128, 1024], F32))
        PSBIG.append(t)
        PSB.append((t[:, 0:512], Buf()))
        PSB.append((t[:, 512:1024], Buf()))

    ident = alloc(top, "ident", [128, 128]); b_ident = Buf()
    identb = alloc(top, "identb", [128, 128], BF16); b_identb = Buf()
    P.dma("sp", ident[:], IN["ident"], writes=[b_ident])
    P.dma("pool", identb[:], IN["ident"], writes=[b_identb])

    def fsize(ap):
        shp = ap.shape
        n = 1
        for v in shp[1:]:
            n *= int(v)
        return n

    def MM(out, lhsT, rhs, st, sp_, R, W, sgc=False):
        c = max(fsize(rhs), 64) / 2.4 + 8.0
        if lhsT.dtype == F32:
            c *= 4.0
        if sgc:
            P.op("pe", lambda e: e.matmul(out, lhsT, rhs, start=st, stop=sp_, skip_group_check=True), R, W, cost=c)
        else:
            P.op("pe", lambda e: e.matmul(out, lhsT, rhs, start=st, stop=sp_), R, W, cost=c)

    def TR(out, in_, idn, R, W):
        P.op("pe", lambda e: e.transpose(out, in_, idn), R + [b_ident], W, cost=230.0)

    def ACT(out, in_, func, R, W, bias=None, scale=None):
        kw = {}
        if bias is not None:
            kw["bias"] = bias
        if scale is not None:
            kw["scale"] = scale
        P.op("act", lambda e: e.activation(out=out, in_=in_, func=func, **kw), R, W, cost=200.0 + fsize(out) / 1.4)

    def V(eng, meth, R, W, **kw):
        o = kw.get("out", kw.get("ap"))
        f = fsize(kw["in_"]) if meth in ("bn_stats", "max") else fsize(o)
        c = (70.0 + f / 0.96) if eng == "dve" else (150.0 + f / 0.7)
        P.op(eng, lambda e: getattr(e, meth)(**kw), R, W, cost=c)

    def evac(eng, out, in_, R, W, scale=None):
        if eng == "act":
            if scale is None:
                ACT(out, in_, AF.Identity, R, W)
            else:
                ACT(out, in_, AF.Identity, R, W, scale=scale)
        else:
            if scale is None:
                V("dve", "tensor_copy", R, W, out=out, in_=in_)
            else:
                V("dve", "tensor_scalar", R, W, out=out, in0=in_, scalar1=scale, scalar2=None, op0=ALU.mult)

    def load_xT(src, r0, ntile, xs, bxs, xT, bxT, psrot):
        n = 128 * ntile
        P.dma("sp", xs[:, 0:ntile, :], src.ap[r0:r0 + n, :].rearrange("(j p) f -> p j f", p=128),
              reads=src.all(lambda k: k[0] <= r0 < k[1] or r0 <= k[0] < r0 + n), writes=[bxs])
        per_bank = 512 // n if n < 512 else 1
        kc = 0
        tog = 0
        while kc < 8:
            pt, pb = psrot.next()
            nk = min(per_bank, 8 - kc)
            for q in range(nk):
                for j in range(ntile):
                    TR(pt[:, q * n + j * 128: q * n + (j + 1) * 128], xs[:, j, (kc + q) * 128:(kc + q + 1) * 128],
                       ident[:], [bxs], [pb])
            evac("act" if tog else "dve", xT[:, kc:kc + nk, :].rearrange("p k t -> p (k t)"), pt[:, 0:nk * n], [pb], [bxT])
            tog ^= 1
            kc += nk

    def load_w_kc(es, name, src_ap, ncol, splits=None):
        w = alloc(es, name, [128, 8, ncol], BF16)
        if splits is None:
            b = Buf()
            for kc in range(8):
                P.dma("pool", w[:, kc, :], src_ap[kc * 128:(kc + 1) * 128, :], writes=[b])
            return w, b
        bufs = []
        for gi in range(len(splits) - 1):
            c0, c1 = splits[gi], splits[gi + 1]
            b = Buf()
            for kc in range(8):
                P.dma("pool", w[:, kc, c0:c1], src_ap[kc * 128:(kc + 1) * 128, c0:c1], writes=[b])
            bufs.append(b)
        return w, bufs

    I32 = mybir.dt.int32

    def rsqrt_small(sd, rstd, tmp, bsm):
        V("dve", "tensor_scalar", [bsm], [bsm], out=tmp[:, 0:1].bitcast(I32), in0=sd[:, 0:1].bitcast(I32), scalar1=1, scalar2=None,
          op0=ALU.logical_shift_right)
        V("dve", "tensor_scalar", [bsm], [bsm], out=rstd[:, 0:1].bitcast(I32), in0=tmp[:, 0:1].bitcast(I32), scalar1=-1.0,
          scalar2=float(0x5f3759df), op0=ALU.mult, op1=ALU.add)
        for _ in range(3):
            V("dve", "tensor_tensor", [bsm], [bsm], out=tmp[:, 1:2], in0=sd[:, 0:1], in1=rstd[:, 0:1], op=ALU.mult)
            V("dve", "scalar_tensor_tensor", [bsm], [bsm], out=tmp[:, 2:3], in0=tmp[:, 1:2], scalar=-0.5, in1=rstd[:, 0:1],
              op0=ALU.mult, op1=ALU.mult)
            V("dve", "scalar_tensor_tensor", [bsm], [bsm], out=rstd[:, 0:1], in0=tmp[:, 2:3], scalar=1.5, in1=rstd[:, 0:1],
              op0=ALU.add, op1=ALU.mult)

    def ln_tail(es_tmp, y, by, gbc, bbc, bgb, out, bout, small, gm_eng="pool"):
        st, mv, sd, rstd, nmr, tmp, bsm = small
        V("dve", "bn_stats", [by], [bsm], out=st[:, 0:6], in_=y[:, 0:512])
        V("dve", "bn_stats", [by, bsm], [bsm], out=st[:, 6:12], in_=y[:, 512:1024])
        V("dve", "bn_aggr", [bsm], [bsm], out=mv[:], in_=st[:, 0:12])
        ACT(sd[:], mv[:, 1:2], AF.Sqrt, [bsm], [bsm], bias=LN_EPS)
        V("dve", "reciprocal", [bsm], [bsm], out=rstd[:], in_=sd[:])
        V("dve", "tensor_scalar", [bsm], [bsm], out=nmr[:], in0=mv[:, 0:1], scalar1=rstd[:, 0:1], scalar2=-1.0,
          op0=ALU.mult, op1=ALU.mult)
        ACT(y[:], y[:], AF.Identity, [by, bsm], [by], bias=nmr[:, 0:1], scale=rstd[:, 0:1])
        V(gm_eng, "tensor_tensor", [by, bgb], [by], out=y[:], in0=y[:], in1=gbc[:], op=ALU.mult)
        V("dve", "tensor_tensor", [by, bgb], [bout], out=out[:], in0=y[:], in1=bbc[:], op=ALU.add)

    def small_ln(es, tag):
        st = alloc(es, f"st{tag}", [128, 12]); mv = alloc(es, f"mv{tag}", [128, 2])
        sd = alloc(es, f"sd{tag}", [128, 1]); rstd = alloc(es, f"rs{tag}", [128, 1]); nmr = alloc(es, f"nm{tag}", [128, 1])
        tmp = alloc(es, f"tm{tag}", [128, 4])
        return (st, mv, sd, rstd, nmr, tmp, Buf())

    def want(name):
        return stages is None or name in stages

    def stage_l0_inproj():
        es = ExitStack()
        W, bWs = load_w_kc(es, "w0in", IN["ev_w_in"], EV_IN, [0, 512, 1024, 1536, 2560])
        bWq, bWk, bWv, bWa = bWs
        xs_r = Rot([(alloc(es, f"xs{i}", [128, 4, D]), Buf()) for i in range(2)])
        xT_r = Rot([(alloc(es, f"xT{i}", [128, 8, 512], BF16), Buf()) for i in range(2)])
        stg_r = Rot([(alloc(es, f"stg{i}", [128, 512], BF16), Buf()) for i in range(6)])
        sig_r = Rot([(alloc(es, f"sig{i}", [128, 512]), Buf()) for i in range(2)])
        zt = alloc(es, "zt", [128, 32], BF16); bz = Buf()
        V("pool", "memset", [], [bz], ap=zt[:], constant=0.0)
        for sq in range(nseq):
            for cc in range(4):
                P.dma("sp", SC["hT0"].ap[sq, cc, :, 0:30], zt[:, 0:30], reads=[bz], writes=[SC["hT0"].b((sq, cc, -1))])
        psT = Rot(PSB[0:2])
        psM = Rot(PSB[2:8])
        tog = 0
        for tc in range(ntok // 512):
            sq = tc // 4
            t0 = (tc % 4) * 512
            xs, bxs = xs_r.next()
            xT, bxT = xT_r.next()
            load_xT(DT(IN["x"]), tc * 512, 4, xs, bxs, xT, bxT, psT)
            for which, col0, dst, scale in (("q", 0, "qT0", 0.125), ("k", 512, "kT0", None)):
                for hp in range(4):
                    pt, pb = psM.next()
                    for kc in range(8):
                        MM(pt[:, :], W[:, kc, col0 + hp * 128: col0 + (hp + 1) * 128], xT[:, kc, :], kc == 0, kc == 7,
                           [bWq if which == "q" else bWk, bxT], [pb])
                    sg, bsg = stg_r.next()
                    evac("act" if tog else "dve", sg[:], pt[:, :], [pb], [bsg], scale=scale)
                    tog ^= 1
                    P.dma("sp", SC[dst].ap[sq, hp, :, t0:t0 + 512], sg[:], reads=[bsg], writes=[SC[dst].b((sq, hp, tc))])
            for cc in range(4):
                pa, pba = psM.next()
                pg, pbg = psM.next()
                for kc in range(8):
                    MM(pa[:, :], W[:, kc, 1536 + cc * 128: 1536 + (cc + 1) * 128], xT[:, kc, :], kc == 0, kc == 7, [bWa, bxT], [pba])
                for kc in range(8):
                    MM(pg[:, :], W[:, kc, 2048 + cc * 128: 2048 + (cc + 1) * 128], xT[:, kc, :], kc == 0, kc == 7, [bWa, bxT], [pbg])
                sgm, bsgm = sig_r.next()
                ACT(sgm[:], pg[:, :], AF.Sigmoid, [pbg], [bsgm])
                sg, bsg = stg_r.next()
                V("dve", "tensor_tensor", [pba, bsgm], [bsg], out=sg[:], in0=pa[:, :], in1=sgm[:], op=ALU.mult)
                P.dma("sp", SC["hT0"].ap[sq, cc, :, 30 + t0:30 + t0 + 512], sg[:], reads=[bsg], writes=[SC["hT0"].b((sq, cc, tc))])
            for j in range(4):
                pt, pb = psM.next()
                for kc in range(8):
                    MM(pt[:, :], xT[:, kc, j * 128:(j + 1) * 128], W[:, kc, 1024:1536], kc == 0, kc == 7, [bWv, bxT], [pb])
                sg, bsg = stg_r.next()
                evac("act" if tog else "dve", sg[:], pt[:, :], [pb], [bsg])
                tog ^= 1
                r0 = tc * 512 + j * 128
                P.dma("sp", SC["v0"].ap[r0:r0 + 128, :], sg[:], reads=[bsg], writes=[SC["v0"].b((sq, r0))])
        P.barrier()
        es.close()

    def stage_l0_sb(fuse_conv=False):
        es = ExitStack()
        negiu = alloc(es, "negiu", [128, 128], BF16); negones = alloc(es, "negones", [128, 128], BF16)
        sbm = alloc(es, "sbm", [128, 128], BF16); bc = Buf()
        P.dma("pool", negiu[:], IN["negiu"], writes=[bc])
        P.dma("pool", sbm[:], IN["sbmask"][:, 0, 0:128], writes=[bc])
        V("pool", "memset", [], [bc], ap=negones[:], constant=-1.0)
        qk_l = []
        for i in range(2):
            qz = alloc(es, f"sbq{i}", [128, 2, S], BF16); kk_ = alloc(es, f"sbk{i}", [128, S], BF16)
            vz = alloc(es, f"sbv{i}", [128, 16, 2, 128], BF16); bb = Buf()
            V("pool", "memset", [], [bb], ap=qz[:], constant=0.0)
            V("pool", "memset", [], [bb], ap=vz[:], constant=0.0)
            qk_l.append((qz, kk_, vz, bb))
        qk_r = Rot(qk_l)
        e1_r = Rot([(alloc(es, f"e1_{i}", [128, 2, 512]), Buf()) for i in range(3)])
        sp_r = Rot([(alloc(es, f"sp_{i}", [128, 2, 512], BF16), Buf()) for i in range(4)])
        w_r = Rot([(alloc(es, f"w_{i}", [128, 2, 512], BF16), Buf()) for i in range(3)])
        S_r = Rot([(alloc(es, f"S_{i}", [128, 2, 512], BF16), Buf()) for i in range(3)])
        o_r = Rot([(alloc(es, f"osg{i}", [128, 512], BF16), Buf()) for i in range(3)])
        psA = Rot([(PSBIG[i], [PSB[2 * i][1], PSB[2 * i + 1][1]]) for i in range(3)])
        psO = Rot([PSB[6]] if fuse_conv else [PSB[6], PSB[7]])
        for sq in range(nseq):
            for hp in range(4):
                q, k, v, bqk = qk_r.next()
                for e in range(2):
                    P.dma("sp", q[64 * e:64 * e + 64, e, :], SC["qT0"].ap[sq, hp, 64 * e:64 * e + 64, :],
                          reads=SC["qT0"].all(lambda kk: kk[0] == sq and kk[1] == hp), writes=[bqk])
                    P.dma("sp", v[:, :, e, 64 * e:64 * e + 64],
                          SC["v0"].ap[sq * S:(sq + 1) * S, hp * 128 + 64 * e:hp * 128 + 64 * e + 64].rearrange("(j p) d -> p j d", p=128),
                          reads=SC["v0"].all(lambda kk: kk[0] == sq), writes=[bqk])
                P.dma("sp", k[:], SC["kT0"].ap[sq, hp], reads=SC["kT0"].all(lambda kk: kk[0] == sq and kk[1] == hp), writes=[bqk])
                for qc in range(4):
                    t0 = qc * 512
                    kbs = list(range(4 * qc + 3, -1, -1))
                    Sprev = None
                    pot, pob = psO.next()
                    for idx, kb in enumerate(kbs):
                        first = idx == 0
                        last = kb == 0
                        diag = kb >= 4 * qc
                        c0 = (kb - 4 * qc) * 128 if diag else 0
                        c1 = c0 + 128
                        s0 = c1 if diag else 0
                        pa, pba = psA.next()
                        pa3 = pa[:, :].rearrange("p (e t) -> p e t", e=2)
                        for e in range(2):
                            MM(pa[:, e * 512 + c0:(e + 1) * 512], k[:, kb * 128:(kb + 1) * 128], q[:, e, t0 + c0:t0 + 512], True, True, [bqk], pba)
                        e1, be1 = e1_r.next()
                        ACT(e1[:, :, c0:512], pa3[:, :, c0:512], AF.Exp, pba, [be1])
                        spt, bsp = sp_r.next()
                        ACT(spt[:, :, c0:512], e1[:, :, c0:512], AF.Ln, [be1], [bsp], bias=1.0)
                        if diag:
                            for e in range(2):
                                V("pool", "tensor_tensor", [bsp, bc], [bsp], out=spt[:, e, c0:c1], in0=spt[:, e, c0:c1], in1=sbm[:], op=ALU.mult)
                        for e in range(2):
                            MM(pa[:, e * 512 + c0:(e + 1) * 512], negiu[:], spt[:, e, c0:512], False, True, [bc, bsp], pba, sgc=True)
                            if not first and s0 < 512:
                                Sp, bSp = Sprev
                                MM(pa[:, e * 512 + s0:(e + 1) * 512], negones[:], Sp[:, e, s0:512], False, True, [bc, bSp], pba, sgc=True)
                        wt, bw = w_r.next()
                        ACT(wt[:, :, c0:512], pa3[:, :, c0:512], AF.Exp, pba, [bw])
                        if diag:
                            for e in range(2):
                                V("pool", "tensor_tensor", [bw, bc], [bw], out=wt[:, e, c0:c1], in0=wt[:, e, c0:c1], in1=sbm[:], op=ALU.mult)
                        for e in range(2):
                            MM(pot[:, c0:512], v[:, kb, e, :], wt[:, e, c0:512], first and e == 0, last and e == 1, [bqk, bw], [pob], sgc=True)
                        if not last:
                            Sn, bSn = S_r.next()
                            if diag:
                                V("dve", "tensor_copy", [bsp], [bSn], out=Sn[:, :, c0:c1], in_=spt[:, :, c0:c1])
                                if not first:
                                    Sp, bSp = Sprev
                                    V("dve", "tensor_tensor", [bsp, bSp, bSn], [bSn], out=Sn[:, :, c1:512], in0=Sp[:, :, c1:512], in1=spt[:, :, c1:512], op=ALU.add)
                            else:
                                Sp, bSp = Sprev
                                V("dve", "tensor_tensor", [bsp, bSp], [bSn], out=Sn[:], in0=Sp[:], in1=spt[:], op=ALU.add)
                            Sprev = (Sn, bSn)
                    og, bog = o_r.next()
                    evac("dve", og[:], pot[:, :], [pob], [bog])
                    P.dma("sp", SC["osbT"].ap[sq, hp, :, t0:t0 + 512], og[:], reads=[bog], writes=[SC["osbT"].b((sq, hp, qc))])
        if fuse_conv:
            P.cur_bias = 1000000
            conv_emit(es, PSB[7], PSB[7], True)
            P.cur_bias = 0
        P.barrier()
        es.close()

    def conv_emit(es, psYb, psSb, exp_only):
        bc = Buf()
        cw = alloc(es, "cw", [32, 512]); vec = alloc(es, "cvec", [12, 128])
        P.dma("sp", cw[0:31, :], IN["ev_conv_w"], writes=[bc])
        for i, nm in enumerate(("ev_conv_b", "ev_conv_ln_g", "ev_conv_ln_b")):
            P.dma("sp", vec[4 * i:4 * i + 4, :], IN[nm].rearrange("(c p) -> c p", p=128), writes=[bc])
        cwT = alloc(es, "cwT", [128, 4 * 31]); vecT = alloc(es, "cvecT", [128, 12]); nvecT = alloc(es, "cnvecT", [128, 12]); bct = Buf()
        pt, pb = psYb
        for cc in range(4):
            TR(pt[:, cc * 31:(cc + 1) * 31], cw[0:31, cc * 128:(cc + 1) * 128], ident[0:31, 0:31], [bc], [pb])
        TR(pt[:, 128:140], vec[0:12, :], ident[0:12, 0:12], [bc], [pb])
        V("dve", "tensor_copy", [pb], [bct], out=cwT[:], in_=pt[:, 0:124])
        V("dve", "tensor_copy", [pb], [bct], out=vecT[:], in_=pt[:, 128:140])
        V("dve", "tensor_scalar", [bct], [bct], out=nvecT[:], in0=vecT[:], scalar1=-1.0, scalar2=None, op0=ALU.mult)
        Dg = alloc(es, "Dg", [128, 4, 31, 128], BF16); bD = [Buf() for _ in range(4)]
        for cc in range(4):
            for w in range(31):
                V("dve", "tensor_scalar", [bct, b_identb], [bD[cc]], out=Dg[:, cc, w, :], in0=identb[:],
                  scalar1=cwT[:, cc * 31 + w: cc * 31 + w + 1], scalar2=None, op0=ALU.mult)
        onesm = alloc(es, "onesm", [128, 128]); bon = Buf()
        V("pool", "memset", [], [bon], ap=onesm[:], constant=1.0 / 512.0)
        hT_r = Rot([(alloc(es, f"hT{i}", [128, 4, S + 30], BF16), Buf()) for i in range(1)])
        y_r = Rot([(alloc(es, f"cy{i}", [128, 4, 512]), Buf()) for i in range(2)])
        ysq_r = Rot([(alloc(es, f"cys{i}", [128, 4, 512]), Buf()) for i in range(2)])
        mean_r = Rot([(alloc(es, f"cmean{i}", [128, 512]), alloc(es, f"crstd{i}", [128, 512]), Buf()) for i in range(2)])
        yn_r = Rot([(alloc(es, f"cyn{i}", [128, 512]), alloc(es, f"cxg{i}", [128, 512]), alloc(es, f"cee{i}", [128, 512]), Buf()) for i in range(2)])
        og_r = Rot([(alloc(es, f"cog{i}", [128, 512], BF16), Buf()) for i in range(3)])
        for sq in range(nseq):
            hT, bh = hT_r.next()
            for cc in range(4):
                P.dma("sp", hT[:, cc, :], SC["hT0"].ap[sq, cc], reads=SC["hT0"].all(lambda kk: kk[0] == sq and kk[1] == cc), writes=[bh])
            for tq in range(4):
                t0 = tq * 512
                y, by = y_r.next()
                ysq, bysq = ysq_r.next()
                for cc in range(4):
                    pt, pb = psYb
                    for w in range(31):
                        MM(pt[:, :], Dg[:, cc, w, :], hT[:, cc, t0 + w:t0 + w + 512], w == 0, w == 30, [bD[cc], bh], [pb])
                    V("dve", "tensor_scalar", [pb, bct], [by], out=y[:, cc, :], in0=pt[:, :], scalar1=vecT[:, cc:cc + 1], scalar2=None, op0=ALU.add)
                    V("pool", "tensor_tensor", [by], [bysq], out=ysq[:, cc, :], in0=y[:, cc, :], in1=y[:, cc, :], op=ALU.mult)
                pm, pbm = psSb
                for cc in range(4):
                    MM(pm[:, :], onesm[:], y[:, cc, :], cc == 0, cc == 3, [bon, by], [pbm])
                mean, rstd, bmr = mean_r.next()
                ACT(mean[:], pm[:, :], AF.Identity, [pbm], [bmr])
                for cc in range(4):
                    MM(pm[:, :], onesm[:], ysq[:, cc, :], cc == 0, cc == 3, [bon, bysq], [pbm])
                V("dve", "tensor_tensor", [bmr], [bmr], out=rstd[:], in0=mean[:], in1=mean[:], op=ALU.mult)
                V("dve", "tensor_tensor", [bmr, pbm], [bmr], out=rstd[:], in0=pm[:, :], in1=rstd[:], op=ALU.subtract)
                if exp_only:
                    ACT(rstd[:], rstd[:], AF.Ln, [bmr], [bmr], bias=LN_EPS)
                    ACT(rstd[:], rstd[:], AF.Exp, [bmr], [bmr], scale=-0.5)
                else:
                    ACT(rstd[:], rstd[:], AF.Sqrt, [bmr], [bmr], bias=LN_EPS)
                    V("dve", "reciprocal", [bmr], [bmr], out=rstd[:], in_=rstd[:])
                for cc in range(4):
                    yn, xg, ee, byn = yn_r.next()
                    V("dve", "tensor_tensor", [by, bmr], [byn], out=yn[:], in0=y[:, cc, :], in1=mean[:], op=ALU.subtract)
                    V("pool", "tensor_tensor", [byn, bmr], [byn], out=yn[:], in0=yn[:], in1=rstd[:], op=ALU.mult)
                    og, bog = og_r.next()
                    if exp_only:
                        ACT(ee[:], yn[:], AF.Exp, [byn, bct], [byn], bias=nvecT[:, 8 + cc:9 + cc], scale=nvecT[:, 4 + cc:5 + cc])
                        V("dve", "tensor_scalar", [byn, bct], [byn], out=xg[:], in0=yn[:], scalar1=vecT[:, 4 + cc:5 + cc],
                          scalar2=vecT[:, 8 + cc:9 + cc], op0=ALU.mult, op1=ALU.add)
                        ACT(ee[:], ee[:], AF.Ln, [byn], [byn], bias=1.0)
                        ACT(ee[:], ee[:], AF.Exp, [byn], [byn], scale=-1.0)
                        V("dve", "tensor_tensor", [byn], [bog], out=og[:], in0=xg[:], in1=ee[:], op=ALU.mult)
                    else:
                        ACT(og[:], yn[:], AF.Silu, [byn, bct], [bog], bias=vecT[:, 8 + cc:9 + cc], scale=vecT[:, 4 + cc:5 + cc])
                    P.dma("sp", SC["ocvT"].ap[sq, cc, :, t0:t0 + 512], og[:], reads=[bog], writes=[SC["ocvT"].b((sq, cc, tq))])

    def stage_l0_conv():
        es = ExitStack()
        conv_emit(es, PSB[1], PSB[2], False)
        P.barrier()
        es.close()

    def stage_outproj(layer, srcA, nA, kA, srcB, wname, xsrc, dst, gname, bname):
        es_ffn = ExitStack()
        Wg_ = alloc(es_ffn, "wg", [128, 8, DFF], BF16); Wu_ = alloc(es_ffn, "wu", [128, 8, DFF], BF16)
        bWg_, bWu_ = Buf(), Buf()
        es = ExitStack()
        bW = Buf()
        WA = alloc(es, "woA", [kA, nA, D], BF16)
        WB = alloc(es, "woB", [128, 4, D], BF16)
        P.dma("pool", WA[:], IN[wname][0:512, :].rearrange("(h d) n -> d h n", d=kA), writes=[bW])
        P.dma("pool", WB[:], IN[wname][512:1024, :].rearrange("(c p) n -> p c n", p=128), writes=[bW])
        P.cur_bias = 300
        for kc in range(8):
            P.dma("pool", Wg_[:, kc, :], IN["ffn_w_gate"][layer][kc * 128:(kc + 1) * 128, :], writes=[bWg_])
            P.dma("pool", Wu_[:, kc, :], IN["ffn_w_up"][layer][kc * 128:(kc + 1) * 128, :], writes=[bWu_])
        P.cur_bias = 0
        pre = (es_ffn, Wg_, bWg_, Wu_, bWu_)
        gbc = alloc(es, "gbc", [128, D]); bbc = alloc(es, "bbc", [128, D]); bgb = Buf()
        P.dma("sp", gbc[:], IN[gname][layer:layer + 1, :].broadcast_to([128, D]), writes=[bgb])
        P.dma("sp", bbc[:], IN[bname][layer:layer + 1, :].broadcast_to([128, D]), writes=[bgb])
        A_r = Rot([(alloc(es, f"lA{i}", [kA, nA, 512], BF16), alloc(es, f"lB{i}", [128, 4, 512], BF16),
                    alloc(es, f"lx{i}", [128, 4, D]), Buf()) for i in range(2)])
        y_r = Rot([(alloc(es, f"ly{i}", [128, D]), Buf()) for i in range(2)])
        o_r = Rot([(alloc(es, f"lo{i}", [128, D]), Buf()) for i in range(2)])
        sm_r = Rot([small_ln(es, i) for i in range(2)])
        psr = Rot([(PSB[0], PSB[1]), (PSB[2], PSB[3]), (PSB[4], PSB[5]), (PSB[6], PSB[7])])
        for tc in range(ntok // 512):
            sq = tc // 4
            t0 = (tc % 4) * 512
            A, B, xs, bl = A_r.next()
            P.dma("sp", A[:], srcA.ap[sq, :, :, t0:t0 + 512].rearrange("h d t -> d h t"),
                  reads=srcA.all(lambda kk: kk[0] == sq), writes=[bl])
            P.dma("sp", B[:], srcB.ap[sq, :, :, t0:t0 + 512].rearrange("c p t -> p c t"),
                  reads=srcB.all(lambda kk: kk[0] == sq), writes=[bl])
            r0 = tc * 512
            P.dma("sp", xs[:], xsrc.ap[r0:r0 + 512, :].rearrange("(j p) f -> p j f", p=128),
                  reads=xsrc.all(lambda kk: r0 <= kk[0] < r0 + 512), writes=[bl])
            for j in range(4):
                banks = psr.next()
                y, by = y_r.next()
                for nh in range(2):
                    pt, pb = banks[nh]
                    n_mm = nA + 4
                    i = 0
                    for h in range(nA):
                        MM(pt[:, :], A[:, h, j * 128:(j + 1) * 128], WA[:, h, nh * 512:(nh + 1) * 512], i == 0, i == n_mm - 1, [bl, bW], [pb])
                        i += 1
                    for c in range(4):
                        MM(pt[:, :], B[:, c, j * 128:(j + 1) * 128], WB[:, c, nh * 512:(nh + 1) * 512], i == 0, i == n_mm - 1, [bl, bW], [pb])
                        i += 1
                    V("dve", "scalar_tensor_tensor", [bl, pb], [by], out=y[:, nh * 512:(nh + 1) * 512], in0=xs[:, j, nh * 512:(nh + 1) * 512],
                      scalar=ALPHA, in1=pt[:, :], op0=ALU.mult, op1=ALU.add)
                o, bo = o_r.next()
                ln_tail(es, y, by, gbc, bbc, bgb, o, bo, sm_r.next(), gm_eng="dve")
                rr = r0 + j * 128
                P.dma("sp", dst.ap[rr:rr + 128, :], o[:], reads=[bo], writes=[dst.b((rr, rr + 128))])
        P.barrier()
        es.close()
        return pre

    def stage_ffn(layer, xsrc, dst, pre=None):
        if pre is None:
            es_ffn = ExitStack()
            Wg, bWg = load_w_kc(es_ffn, "wg", IN["ffn_w_gate"][layer], DFF, None)
            Wu, bWu = load_w_kc(es_ffn, "wu", IN["ffn_w_up"][layer], DFF, None)
        else:
            es_ffn, Wg, bWg, Wu, bWu = pre
        es = ExitStack()
        Wd = alloc(es, "wd", [128, NFC, D], BF16); bWd = Buf()
        for i in range(2):
            P.dma("pool", Wd[:, 11 * i:11 * i + 11, :], IN["ffn_w_down"][layer, 11 * i * 128:(11 * i + 11) * 128, :].rearrange("(c p) n -> p c n", p=128),
                  writes=[bWd])
        gbc = alloc(es, "fgbc", [128, D]); bbc = alloc(es, "fbbc", [128, D]); bgb = Buf()
        P.dma("sp", gbc[:], IN["ln2_g"][layer:layer + 1, :].broadcast_to([128, D]), writes=[bgb])
        P.dma("sp", bbc[:], IN["ln2_b"][layer:layer + 1, :].broadcast_to([128, D]), writes=[bgb])
        xs_r = Rot([(alloc(es, f"fx{i}", [128, 2, D]), Buf()) for i in range(2)])
        xT_r = Rot([(alloc(es, f"fxT{i}", [128, 8, 256], BF16), Buf()) for i in range(2)])
        hT = alloc(es, "fh", [128, NFC, 256], BF16); bh = Buf()
        s_r = Rot([(alloc(es, f"fs{i}", [128, 256]), Buf()) for i in range(3)])
        y_r = Rot([(alloc(es, f"fy{i}", [128, D]), Buf()) for i in range(2)])
        o_r = Rot([(alloc(es, f"fo{i}", [128, D]), Buf()) for i in range(1)])
        sm_r = Rot([small_ln(es, f"f{i}") for i in range(2)])
        psG = Rot(PSB[0:4])
        for tc in range(ntok // 256):
            r0 = tc * 256
            xs, bxs = xs_r.next()
            xT, bxT = xT_r.next()
            load_xT(xsrc, r0, 2, xs, bxs, xT, bxT, psG)
            for fc in range(NFC):
                pg, pbg = psG.next()
                pu, pbu = psG.next()
                for kc in range(8):
                    MM(pg[:, 0:256], Wg[:, kc, fc * 128:(fc + 1) * 128], xT[:, kc, :], kc == 0, kc == 7, [bWg, bxT], [pbg])
                for kc in range(8):
                    MM(pu[:, 0:256], Wu[:, kc, fc * 128:(fc + 1) * 128], xT[:, kc, :], kc == 0, kc == 7, [bWu, bxT], [pbu])
                st, bst = s_r.next()
                ACT(st[:], pg[:, 0:256], AF.Silu, [pbg], [bst])
                V("dve", "tensor_tensor", [bst, pbu], [bh], out=hT[:, fc, :], in0=pu[:, 0:256], in1=st[:], op=ALU.mult)
            for j in range(2):
                y, by = y_r.next()
                for nh in range(2):
                    pt, pb = PSB[4 + 2 * j + nh]
                    for fc in range(NFC):
                        MM(pt[:, :], hT[:, fc, j * 128:(j + 1) * 128], Wd[:, fc, nh * 512:(nh + 1) * 512], fc == 0, fc == NFC - 1, [bh, bWd], [pb])
                    V("dve", "scalar_tensor_tensor", [bxs, pb], [by], out=y[:, nh * 512:(nh + 1) * 512], in0=xs[:, j, nh * 512:(nh + 1) * 512],
                      scalar=ALPHA, in1=pt[:, :], op0=ALU.mult, op1=ALU.add)
                o, bo = o_r.next()
                ln_tail(es, y, by, gbc, bbc, bgb, o, bo, sm_r.next())
                rr = r0 + j * 128
                P.dma("sp", dst.ap[rr:rr + 128, :], o[:], reads=[bo], writes=[dst.b((rr, rr + 128))])
        P.barrier()
        es.close()
        es_ffn.close()

    def stage_l1_inproj(xsrc):
        es = ExitStack()
        W, bWs = load_w_kc(es, "w1in", IN["od_w_in"], OD_IN, [0, 512, 1304, 1816, 2328])
        bWq, bWkv, bWu_, bWv_ = bWs
        bc = Buf()
        wsf = alloc(es, "wsf", [128, 4, 128]); tril = alloc(es, "tril", [128, 128])
        P.dma("sp", wsf[:], IN["od_gmlp_ws"].rearrange("g t s -> t g s"), writes=[bc])
        P.dma("sp", tril[:], IN["tril"], writes=[bc])
        bsr = alloc(es, "bsr", [4, 128])
        P.dma("sp", bsr[:], IN["od_gmlp_bs"], writes=[bc])
        wsT = alloc(es, "wsT", [128, 4, 128], BF16); bsT = alloc(es, "bsT", [128, 4]); bc2 = Buf()
        pt, pb = PSB[0]
        for g in range(4):
            TR(pt[:, g * 128:(g + 1) * 128], wsf[:, g, :], ident[:], [bc], [pb])
        for g in range(4):
            V("dve", "tensor_tensor", [pb, bc], [bc2], out=wsT[:, g, :], in0=pt[:, g * 128:(g + 1) * 128], in1=tril[:], op=ALU.mult)
        pt2, pb2 = PSB[1]
        TR(pt2[:, 0:4], bsr[0:4, :], ident[0:4, 0:4], [bc], [pb2])
        V("dve", "tensor_copy", [pb2], [bc2], out=bsT[:], in_=pt2[:, 0:4])
        ggb = alloc(es, "ggb", [128, 512]); gbb = alloc(es, "gbb", [128, 512])
        P.dma("sp", ggb[:], IN["od_gmlp_ln_g"].rearrange("(o n) -> o n", o=1).broadcast_to([128, 512]), writes=[bc2])
        P.dma("sp", gbb[:], IN["od_gmlp_ln_b"].rearrange("(o n) -> o n", o=1).broadcast_to([128, 512]), writes=[bc2])
        xs_r = Rot([(alloc(es, f"xs{i}", [128, 4, D]), Buf()) for i in range(2)])
        xT_r = Rot([(alloc(es, f"xT{i}", [128, 8, 512], BF16), Buf()) for i in range(2)])
        stg_r = Rot([(alloc(es, f"stg{i}", [128, 512], BF16), Buf()) for i in range(6)])
        va_r = []
        for i in range(3):
            t = alloc(es, f"va{i}", [128, 2, 130], BF16)
            b = Buf()
            V("pool", "memset", [], [b], ap=t[:], constant=1.0)
            va_r.append((t, b))
        va_r = Rot(va_r)
        gt_r = Rot([(alloc(es, f"gt{i}", [128, 24]), Buf()) for i in range(3)])
        gu_r = Rot([(alloc(es, f"gu{i}", [128, 512]), Buf()) for i in range(2)])
        gv_r = Rot([(alloc(es, f"gv{i}", [128, 512]), Buf()) for i in range(2)])
        vn_r = Rot([(alloc(es, f"vn{i}", [128, 512], BF16), Buf()) for i in range(2)])
        om_r = Rot([(alloc(es, f"om{i}", [128, 512]), Buf()) for i in range(2)])
        omT_r = Rot([(alloc(es, f"omT{i}", [128, 4, 512], BF16), Buf()) for i in range(2)])
        sm_r = Rot([small_ln(es, f"g{i}") for i in range(2)])
        psT = Rot(PSB[0:2])
        psM = Rot(PSB[2:8])
        tog = 0
        for tc in range(ntok // 512):
            sq = tc // 4
            t0 = (tc % 4) * 512
            xs, bxs = xs_r.next()
            xT, bxT = xT_r.next()
            load_xT(xsrc, tc * 512, 4, xs, bxs, xT, bxT, psT)
            fm = [("q1T", hp, hp * 128, 0.125) for hp in range(4)] + [("kcT", None, 512, None), ("vcT", None, 640, None),
                                                                       ("ksT", None, 768, None), ("kwT", None, 1024, None)]
            for dstn, hp, col0, scale in fm:
                pt, pb = psM.next()
                for kc in range(8):
                    MM(pt[:, :], W[:, kc, col0:col0 + 128], xT[:, kc, :], kc == 0, kc == 7, [bWq if hp is not None else bWkv, bxT], [pb])
                sg, bsg = stg_r.next()
                evac("act" if tog else "dve", sg[:], pt[:, :], [pb], [bsg], scale=scale)
                tog ^= 1
                if hp is None:
                    P.dma("sp", SC[dstn].ap[sq, :, t0:t0 + 512], sg[:], reads=[bsg], writes=[SC[dstn].b((sq, tc))])
                else:
                    P.dma("sp", SC[dstn].ap[sq, hp, :, t0:t0 + 512], sg[:], reads=[bsg], writes=[SC[dstn].b((sq, hp, tc))])
            omT, bomT = omT_r.next()
            for j in range(4):
                r0 = tc * 512 + j * 128
                xTj = lambda kc: xT[:, kc, j * 128:(j + 1) * 128]
                pt, pb = psM.next()
                for (c0, n, o0) in ((896, 128, 0), (1152, 128, 128), (1280, 24, 256)):
                    for kc in range(8):
                        MM(pt[:, o0:o0 + n], xT[:, kc, j * 128:(j + 1) * 128], W[:, kc, c0:c0 + n], kc == 0, kc == 7, [bWkv, bxT], [pb])
                va, bva = va_r.next()
                V("dve", "tensor_copy", [pb], [bva], out=va[:, 0, :].rearrange("p (g d) -> p g d", g=2)[:, :, 0:64],
                  in_=pt[:, 0:128].rearrange("p (g d) -> p g d", g=2))
                V("dve", "tensor_copy", [pb], [bva], out=va[:, 1, :].rearrange("p (g d) -> p g d", g=2)[:, :, 0:64],
                  in_=pt[:, 128:256].rearrange("p (g d) -> p g d", g=2))
                gt, bgt = gt_r.next()
                ACT(gt[:], pt[:, 256:280], AF.Sigmoid, [pb], [bgt])
                P.dma("sp", SC["vsA"].ap[r0:r0 + 128, :], va[:, 0, :], reads=[bva], writes=[SC["vsA"].b((sq, r0))])
                P.dma("sp", SC["vwA"].ap[r0:r0 + 128, :], va[:, 1, :], reads=[bva], writes=[SC["vwA"].b((sq, r0))])
                P.dma("sp", SC["gates"].ap[r0:r0 + 128, :], gt[:], reads=[bgt], writes=[SC["gates"].b((sq, r0))])
                pu, pbu = psM.next()
                pv, pbv = psM.next()
                for kc in range(8):
                    MM(pu[:, :], xT[:, kc, j * 128:(j + 1) * 128], W[:, kc, 1304:1816], kc == 0, kc == 7, [bWu_, bxT], [pbu])
                for kc in range(8):
                    MM(pv[:, :], xT[:, kc, j * 128:(j + 1) * 128], W[:, kc, 1816:2328], kc == 0, kc == 7, [bWv_, bxT], [pbv])
                gu, bgu = gu_r.next()
                gv, bgv = gv_r.next()
                ACT(gu[:], pu[:, :], AF.Gelu_apprx_tanh, [pbu], [bgu])
                ACT(gv[:], pv[:, :], AF.Gelu_apprx_tanh, [pbv], [bgv])
                st, mv, sd, rstd, nmr, tmp, bsm = sm_r.next()
                V("dve", "bn_stats", [bgv], [bsm], out=st[:, 0:6], in_=gv[:])
                V("dve", "bn_aggr", [bsm], [bsm], out=mv[:], in_=st[:, 0:6])
                ACT(sd[:], mv[:, 1:2], AF.Sqrt, [bsm], [bsm], bias=LN_EPS)
                V("dve", "reciprocal", [bsm], [bsm], out=rstd[:], in_=sd[:])
                V("dve", "tensor_scalar", [bsm], [bsm], out=nmr[:], in0=mv[:, 0:1], scalar1=rstd[:, 0:1], scalar2=-1.0,
                  op0=ALU.mult, op1=ALU.mult)
                ACT(gv[:], gv[:], AF.Identity, [bgv, bsm], [bgv], bias=nmr[:, 0:1], scale=rstd[:, 0:1])
                V("dve", "tensor_tensor", [bgv, bc2], [bgv], out=gv[:], in0=gv[:], in1=ggb[:], op=ALU.mult)
                vn, bvn = vn_r.next()
                V("dve", "tensor_tensor", [bgv, bc2], [bvn], out=vn[:], in0=gv[:], in1=gbb[:], op=ALU.add)
                pm, pbm = psM.next()
                for g in range(4):
                    MM(pm[:, g * 128:(g + 1) * 128], wsT[:, g, :], vn[:, g * 128:(g + 1) * 128], True, True, [bc2, bvn], [pbm])
                om, bom = om_r.next()
                for g in range(4):
                    V("dve", "scalar_tensor_tensor", [pbm, bgu, bc2], [bom], out=om[:, g * 128:(g + 1) * 128], in0=pm[:, g * 128:(g + 1) * 128],
                      scalar=bsT[:, g:g + 1], in1=gu[:, g * 128:(g + 1) * 128], op0=ALU.add, op1=ALU.mult)
                pt, pb = psM.next()
                for cc in range(4):
                    TR(pt[:, cc * 128:(cc + 1) * 128], om[:, cc * 128:(cc + 1) * 128], ident[:], [bom], [pb])
                ACT(omT[:, :, j * 128:(j + 1) * 128], pt[:, :].rearrange("p (c t) -> p c t", c=4), AF.Identity, [pb], [bomT])
            P.dma("sp", SC["omlpT"].ap[sq, :, :, t0:t0 + 512].rearrange("c p t -> p c t"), omT[:], reads=[bomT],
                  writes=[SC["omlpT"].b((sq, tc))])
        P.barrier()
        es.close()

    def pro_nsa():
        es = ExitStack()
        bc = Buf()
        trin = alloc(es, "trin", [128, 2, 128], BF16)
        cmn = alloc(es, "cmn", [128, S], BF16); ematn = alloc(es, "ematn", [128, 16, 128], BF16)
        sadd = alloc(es, "sadd", [128, 16, 32])
        P.dma("pool", trin[:], IN["trineg"], writes=[bc])
        P.dma("pool", cmn[:], IN["cmneg"], writes=[bc])
        P.dma("pool", ematn[:], IN["ematn"], writes=[bc])
        P.dma("sp", sadd[:], IN["scoreadd"], writes=[bc])
        W1 = {}
        W2 = {}
        cb = {}
        for kv, (w1n, w2n, posn) in (("k", ("od_cmpk_w1", "od_cmpk_w2", "od_cmpk_pos")), ("v", ("od_cmpv_w1", "od_cmpv_w2", "od_cmpv_pos"))):
            w1 = alloc(es, f"w1{kv}", [128, 32, 128], BF16)
            w2 = alloc(es, f"w2{kv}", [128, 128], BF16)
            bw = Buf()
            V("pool", "memset", [], [bw], ap=w1[:], constant=0.0)
            V("pool", "memset", [], [bw], ap=w2[:], constant=0.0)
            for g in range(2):
                P.dma("pool", w1[64 * g:64 * g + 64, :, 64 * g:64 * g + 64], IN[w1n].rearrange("(l d) e -> d l e", d=64), writes=[bw])
                P.dma("pool", w2[64 * g:64 * g + 64, 64 * g:64 * g + 64], IN[w2n], writes=[bw])
            posf = alloc(es, f"posf{kv}", [32, 128])
            for g in range(2):
                P.dma("sp", posf[:, 64 * g:64 * g + 64], IN[posn], writes=[bw])
            pt, pb = PSB[0]
            TR(pt[:, 0:32], posf[0:32, :], ident[0:32, 0:32], [bw], [pb])
            posT = alloc(es, f"posT{kv}", [128, 32], BF16)
            V("dve", "tensor_copy", [pb], [bw], out=posT[:], in_=pt[:, 0:32])
            pt2, pb2 = PSB[1]
            for l in range(32):
                MM(pt2[:, 0:1], w1[:, l, :], posT[:, l:l + 1], l == 0, l == 31, [bw], [pb2])
            cbias = alloc(es, f"cb{kv}", [128, 1])
            V("dve", "tensor_copy", [pb2], [bw], out=cbias[:], in_=pt2[:, 0:1])
            W1[kv] = (w1, bw)
            W2[kv] = w2
            cb[kv] = cbias
        raug = alloc(es, "raug", [128, 2, 97], BF16); braug = Buf()
        V("pool", "memset", [], [braug], ap=raug[:], constant=1.0)
        for g in range(2):
            P.dma("pool", raug[:, g, 65:97], IN["overlap"], writes=[braug])
        kcmpT = alloc(es, "kcmpT", [128, 128], BF16); bkc = Buf()
        vsA = alloc(es, "vsA", [128, 16, 2, 128], BF16); vwA = alloc(es, "vwA", [128, 16, 2, 128], BF16)
        bld = Buf()
        V("pool", "memset", [], [bld], ap=vsA[:], constant=0.0)
        V("pool", "memset", [], [bld], ap=vwA[:], constant=0.0)
        V("pool", "memset", [], [bkc], ap=kcmpT[:], constant=0.0)
        selT = alloc(es, "selT", [128, 2, S], BF16); bsel = [[Buf() for _ in range(4)] for _ in range(2)]
        V("pool", "memset", [], [bsel[g][t] for g in range(2) for t in range(4)], ap=selT[:], constant=0.0)
        return dict(es=es, bc=bc, trin=trin, cmn=cmn, ematn=ematn, sadd=sadd, W1=W1, W2=W2, cb=cb, raug=raug, braug=braug,
                    kcmpT=kcmpT, bkc=bkc, vsA=vsA, vwA=vwA, bld=bld, selT=selT, bsel=bsel)

    def stage_l1_nsa(pn):
        es = ExitStack()
        bc, trin, cmn, ematn, sadd, W1, W2, cb = pn["bc"], pn["trin"], pn["cmn"], pn["ematn"], pn["sadd"], pn["W1"], pn["W2"], pn["cb"]
        raug, braug, kcmpT, bkc, vsA, vwA, bld, selT, bsel = (pn["raug"], pn["braug"], pn["kcmpT"], pn["bkc"], pn["vsA"], pn["vwA"],
                                                              pn["bld"], pn["selT"], pn["bsel"])
        gel_r = Rot([(alloc(es, f"gel{i}", [128, 128], BF16), Buf()) for i in range(2)])
        q1 = alloc(es, "q1", [128, 8, S], BF16); kcT = alloc(es, "kcT", [128, S], BF16); vcT = alloc(es, "vcT", [128, S], BF16)
        ksT = alloc(es, "ksT", [128, S], BF16); kwT = alloc(es, "kwT", [128, S], BF16)
        gts = alloc(es, "gts", [128, 16, 24])
        V("pool", "memset", [], [bld], ap=q1[:, 0:4, :], constant=0.0)
        V("dve", "memset", [], [bld], ap=q1[:, 4:8, :], constant=0.0)
        O = alloc(es, "O", [128, 16, 512]); bO = [[Buf() for _ in range(8)] for _ in range(16)]
        imp = alloc(es, "imp", [128, 16, 2, 32]); bimp = [[Buf() for _ in range(2)] for _ in range(16)]
        eT_r = Rot([(alloc(es, f"eT{i}", [128, 512], BF16), Buf()) for i in range(3)])
        pT_r = Rot([(alloc(es, f"pT{i}", [128, 512], BF16), Buf()) for i in range(6)])
        oT_r = Rot([(alloc(es, f"oTs{i}", [65, 512]), Buf()) for i in range(3)])
        sm_r = Rot([(alloc(es, f"rd{i}", [128, 12]), Buf()) for i in range(8)])
        tm_r = Rot([(alloc(es, f"tmf{i}", [128, 4, 64]), Buf()) for i in range(3)])
        ti_r = Rot([(alloc(es, f"tif{i}", [128, 4, 32]), Buf()) for i in range(3)])
        sc_r = Rot([(alloc(es, f"sc{i}", [128, 32]), alloc(es, f"t8{i}", [128, 8]), alloc(es, f"sm{i}", [128, 32]), Buf()) for i in range(4)])
        onT_r = Rot([(alloc(es, f"onT{i}", [128, 4, 512], BF16), Buf()) for i in range(2)])
        TINY = 1e-30

        def finalize(acc4, bacc, hh, branch, tq, accumulate):
            j0 = 4 * tq
            rd, brd = sm_r.next()
            bos = [bO[j0 + j][hh] for j in range(4)]
            V("dve", "tensor_scalar", [bacc], [brd], out=rd[:, 0:4], in0=acc4[:, :, 64], scalar1=TINY, scalar2=None, op0=ALU.max)
            V("dve", "reciprocal", [brd], [brd], out=rd[:, 4:8], in_=rd[:, 0:4])
            V("dve", "tensor_tensor", [brd, bld], [brd], out=rd[:, 8:12], in0=rd[:, 4:8], in1=gts[:, j0:j0 + 4, 3 * hh + branch], op=ALU.mult)
            Oview = O[:, j0:j0 + 4, hh * 64:(hh + 1) * 64]
            gb = rd[:, 8:12].unsqueeze(2).broadcast_to([128, 4, 64])
            if accumulate:
                tm, btm = tm_r.next()
                V("dve", "tensor_tensor", [bacc, brd], [btm], out=tm[:], in0=acc4[:, :, 0:64], in1=gb, op=ALU.mult)
                V("pool", "tensor_tensor", [btm] + bos, bos, out=Oview, in0=Oview, in1=tm[:], op=ALU.add)
            else:
                V("dve", "tensor_tensor", [bacc, brd], bos, out=Oview, in0=acc4[:, :, 0:64], in1=gb, op=ALU.mult)
            if branch == 0:
                g = hh // 4
                bis = [bimp[j0 + j][g] for j in range(4)]
                rb = rd[:, 4:8].unsqueeze(2).broadcast_to([128, 4, 32])
                Iview = imp[:, j0:j0 + 4, g, :]
                if hh % 4 == 0:
                    V("dve", "tensor_tensor", [bacc, brd], bis, out=Iview, in0=acc4[:, :, 65:97], in1=rb, op=ALU.mult)
                else:
                    ti, bti = ti_r.next()
                    V("dve", "tensor_tensor", [bacc, brd], [bti], out=ti[:], in0=acc4[:, :, 65:97], in1=rb, op=ALU.mult)
                    V("pool", "tensor_tensor", [bti] + bis, bis, out=Iview, in0=Iview, in1=ti[:], op=ALU.add)

        def finalize_T(pacc, bpacc, hh, branch, tq, psT, tog):
            oTs, boTs = oT_r.next()
            evac("act" if tog else "dve", oTs[:], pacc[0:65, :], [bpacc], [boTs])
            pt, pb = psT
            for j in range(4):
                TR(pt[:, j * 65:(j + 1) * 65], oTs[0:65, j * 128:(j + 1) * 128], ident[0:65, 0:65], [boTs], [pb])
            finalize(pt[:, 0:260].rearrange("p (j c) -> p j c", c=65), pb, hh, branch, tq, True)

        for sq in range(nseq):
            for hh in range(8):
                g_, r_ = hh // 4, hh % 4
                P.dma("sp", q1[64 * g_:64 * g_ + 64, hh, :], SC["q1T"].ap[sq, r_, 64 * g_:64 * g_ + 64, :],
                      reads=SC["q1T"].all(lambda kk: kk[0] == sq and kk[1] == r_), writes=[bld])
            for nm, t in (("kcT", kcT), ("vcT", vcT), ("ksT", ksT), ("kwT", kwT)):
                P.dma("sp", t[:], SC[nm].ap[sq], reads=SC[nm].all(lambda kk: kk[0] == sq), writes=[bld])
            for nm, t in (("vsA", vsA), ("vwA", vwA)):
                for g_ in range(2):
                    P.dma("sp", t[:, :, g_, 0:65], SC[nm].ap[sq * S:(sq + 1) * S, 65 * g_:65 * g_ + 65].rearrange("(j p) d -> p j d", p=128),
                          reads=SC[nm].all(lambda kk: kk[0] == sq), writes=[bld])
            P.dma("sp", gts[:], SC["gates"].ap[sq * S:(sq + 1) * S, :].rearrange("(j p) d -> p j d", p=128),
                  reads=SC["gates"].all(lambda kk: kk[0] == sq), writes=[bld])
            for kv, src in (("k", kcT), ("v", vcT)):
                w1, bw = W1[kv]
                pt, pb = PSB[0] if kv == "k" else PSB[1]
                for l in range(32):
                    MM(pt[:, 0:127], w1[:, l, :], src[:, l:l + 2017:16], l == 0, l == 31, [bw, bld], [pb])
                gel, bgel = gel_r.next()
                ACT(gel[:, 0:127], pt[:, 0:127], AF.Gelu_apprx_tanh, [pb, bw], [bgel], bias=cb[kv][:, 0:1])
                pt2, pb2 = PSB[2] if kv == "k" else PSB[3]
                if kv == "k":
                    MM(pt2[:, 0:127], W2[kv][:], gel[:, 0:127], True, True, [bw, bgel], [pb2])
                    V("dve", "tensor_copy", [pb2, bkc], [bkc], out=kcmpT[:, 0:127], in_=pt2[:, 0:127])
                else:
                    MM(pt2[0:127, 0:128], gel[:, 0:127], W2[kv][:], True, True, [bw, bgel], [pb2])
                    V("dve", "tensor_copy", [pb2], [braug], out=raug[0:127, :, 0:64], in_=pt2[0:127, 0:128].rearrange("p (g d) -> p g d", g=2))
            psS = Rot(PSB[0:3])
            psA = Rot(PSB[3:6])
            for tq in range(4):
                t0 = tq * 512
                for hh in range(8):
                    g, r = hh // 4, hh % 4
                    pr = slice(64 * g, 64 * g + 64)
                    pt, pb = psS.next()
                    MM(pt[:, :], kcmpT[:, :], q1[:, hh, t0:t0 + 512], True, False, [bkc, bld], [pb])
                    MM(pt[:, :], identb[:], cmn[:, t0:t0 + 512], False, True, [bc, b_identb], [pb])
                    eT, beT = eT_r.next()
                    ACT(eT[:, :], pt[:, :], AF.Exp, [pb], [beT])
                    pa, pba = psA.next()
                    for j in range(4):
                        MM(pa[:, j * 97:(j + 1) * 97], eT[:, j * 128:(j + 1) * 128], raug[:, g, :], True, True, [beT, braug], [pba])
                    finalize(pa[:, 0:388].rearrange("p (j c) -> p j c", c=97), pba, hh, 0, tq, False)
                for g in range(2):
                    pt, pb = PSB[6 + g]
                    for j in range(4):
                        jj = 4 * tq + j
                        sc, t8, smk, bs_ = sc_r.next()
                        V("dve", "tensor_tensor", [bimp[jj][g], bc], [bs_], out=sc[:], in0=imp[:, jj, g, :], in1=sadd[:, jj, :], op=ALU.add)
                        V("dve", "max", [bs_], [bs_], out=t8[:], in_=sc[:])
                        V("dve", "tensor_scalar", [bs_], [bs_], out=smk[:], in0=sc[:], scalar1=t8[:, 7:8], scalar2=None, op0=ALU.is_lt)
                        TR(pt[0:32, j * 128:(j + 1) * 128], smk[:], ident[:], [bs_], [pb])
                    V("dve", "tensor_copy", [pb], [bsel[g][tq]], out=selT[0:32, g, t0:t0 + 512], in_=pt[0:32, :])
            psS = Rot(PSB[0:3])
            psT = PSB[7]
            tog = 0
            for tq in range(4):
                t0 = tq * 512
                for g in range(2):
                    pr = slice(64 * g, 64 * g + 64)
                    nkb = 4 * tq + 4
                    for kb in range(nkb):
                        rel = kb - 4 * tq
                        c0 = max(0, rel) * 128
                        for r in range(4):
                            pt, pb = psS.next()
                            MM(pt[:, c0:512], ksT[:, kb * 128:(kb + 1) * 128], q1[:, 4 * g + r, t0 + c0:t0 + 512], True, False, [bld], [pb])
                            MM(pt[:, c0:512], ematn[:, kb, :], selT[:, g, t0 + c0:t0 + 512], False, rel < 0, [bc, bsel[g][tq]], [pb])
                            if rel >= 0:
                                MM(pt[:, c0:c0 + 128], identb[:], trin[:, 0, :], False, True, [bc, b_identb], [pb], sgc=True)
                            pT, bpT = pT_r.next()
                            ACT(pT[:, c0:512], pt[:, c0:512], AF.Exp, [pb], [bpT])
                            pa, pba = PSB[3 + r]
                            MM(pa[:, c0:512], vsA[:, kb, g, :], pT[:, c0:512], kb == 0, kb == nkb - 1, [bpT, bld], [pba], sgc=True)
                    for r in range(4):
                        pa, pba = PSB[3 + r]
                        finalize_T(pa, pba, 4 * g + r, 1, tq, psT, tog)
                        tog ^= 1
            psS = Rot(PSB[0:3])
            psA = Rot(PSB[3:7])
            for tq in range(4):
                t0 = tq * 512
                for hh in range(8):
                    g, r = hh // 4, hh % 4
                    pr = slice(64 * g, 64 * g + 64)
                    pa, pba = psA.next()
                    kb_lo = max(0, 4 * tq - 4)
                    for kb in range(kb_lo, 4 * tq + 4):
                        rel = kb - (4 * tq - 4)
                        jlo = max(0, rel - 4)
                        jhi = min(3, rel)
                        c0, c1 = jlo * 128, (jhi + 1) * 128
                        pt, pb = psS.next()
                        MM(pt[:, c0:c1], kwT[:, kb * 128:(kb + 1) * 128], q1[:, hh, t0 + c0:t0 + c1], True, False, [bld], [pb])
                        if rel <= 3:
                            MM(pt[:, rel * 128:(rel + 1) * 128], identb[:], trin[:, 1, :], False, True, [bc, b_identb], [pb], sgc=True)
                        else:
                            MM(pt[:, (rel - 4) * 128:(rel - 3) * 128], identb[:], trin[:, 0, :], False, True, [bc, b_identb], [pb], sgc=True)
                        pT, bpT = pT_r.next()
                        ACT(pT[:, c0:c1], pt[:, c0:c1], AF.Exp, [pb], [bpT])
                        MM(pa[:, c0:c1], vwA[:, kb, g, :], pT[:, c0:c1], kb == kb_lo, kb == 4 * tq + 3, [bpT, bld], [pba], sgc=True)
                    finalize_T(pa, pba, hh, 2, tq, psT, tog)
                    tog ^= 1
            psTr = Rot(PSB[0:4])
            tog = 0
            for tq in range(4):
                onT, bon = onT_r.next()
                for j in range(4):
                    jj = 4 * tq + j
                    pt, pb = psTr.next()
                    for cc in range(4):
                        TR(pt[:, cc * 128:(cc + 1) * 128], O[:, jj, cc * 128:(cc + 1) * 128], ident[:], bO[jj][2 * cc:2 * cc + 2], [pb])
                    evac("act" if tog else "dve", onT[:, :, j * 128:(j + 1) * 128], pt[:, :].rearrange("p (c t) -> p c t", c=4), [pb], [bon])
                    tog ^= 1
                P.dma("sp", SC["onsaT"].ap[sq, :, :, tq * 512:(tq + 1) * 512].rearrange("c p t -> p c t"), onT[:], reads=[bon],
                      writes=[SC["onsaT"].b((sq, tq))])
        P.barrier()
        es.close()
        pn["es"].close()

    P.flush()
    if want("l0_inproj"):
        stage_l0_inproj()
    if want("l0_sb") and want("l0_conv"):
        stage_l0_sb(fuse_conv=True)
    else:
        if want("l0_sb"):
            stage_l0_sb()
        if want("l0_conv"):
            stage_l0_conv()
    xin = DT(IN["x"])
    pre0 = None
    if want("l0_out"):
        pre0 = stage_outproj(0, SC["osbT"], 4, 128, SC["ocvT"], "ev_w_out", xin, SC["x1"], "ln1_g", "ln1_b")
        if not want("l0_ffn"):
            pre0[0].close()
    if want("l0_ffn"):
        stage_ffn(0, SC["x1"], SC["x2"], pre0)
    pn = None
    if want("l1_nsa"):
        P.cur_bias = 1500
        pn = pro_nsa()
        P.cur_bias = 0
    if want("l1_inproj"):
        stage_l1_inproj(SC["x2"])
    if want("l1_nsa"):
        stage_l1_nsa(pn)
    pre1 = None
    if want("l1_out"):
        pre1 = stage_outproj(1, SC["onsaT"], 4, 128, SC["omlpT"], "od_w_out", SC["x2"], SC["x3"], "ln1_g", "ln1_b")
        if not want("l1_ffn"):
            pre1[0].close()
    if want("l1_ffn"):
        stage_ffn(1, SC["x3"], out_dt, pre1)
    outs = list(out_dt.all())
    for nm in debug_outs:
        outs += SC[nm].all()
    P.finish(outs)
    top.close()
    return nc, P


def prep_weights(inputs):
    w = {}
    for k in WEIGHT_SHAPES:
        a = np.asarray(inputs[k], dtype=np.float32)
        if k.startswith("ev_") or k.startswith("od_"):
            a = a[0]
        w[k] = np.ascontiguousarray(a)
    wi = w["od_w_in"].copy()
    q = wi[:, 0:512].reshape(D, 2, 4, 64)
    wi[:, 0:512] = q.transpose(0, 2, 1, 3).reshape(D, 512)
    w["od_w_in"] = wi
    return w


_CACHE = {}


def kernel(**inputs):
    x = np.asarray(inputs["x"], dtype=np.float32)
    B = x.shape[0]
    per = B // NCORES
    if "nc" not in _CACHE:
        _CACHE["nc"] = build(per * S)[0]
    nc = _CACHE["nc"]
    w = prep_weights(inputs)
    consts = host_consts()
    in_maps = []
    for c in range(NCORES):
        m = {"x": np.ascontiguousarray(x[c * per:(c + 1) * per].reshape(per * S, D))}
        m.update(w)
        for k, v in consts.items():
            m["c_" + k] = v
        in_maps.append(m)
    res = run_bass_kernel_spmd(nc, in_maps, core_ids=list(range(NCORES)))
    out = np.concatenate([np.asarray(r["out"]).reshape(per, S, D) for r in res.results], axis=0)
    return out.astype(np.float32)
```

```python
import numpy as np
from contextlib import ExitStack
import concourse.bass as bass
import concourse.mybir as mybir
from concourse.bass_utils import run_bass_kernel_spmd

F32 = mybir.dt.float32
BF16 = mybir.dt.bfloat16
AF = mybir.ActivationFunctionType
ALU = mybir.AluOpType

SEM_LIMIT = 30000
NDMA_SEM = 12
NCORES = 8
S = 2048
D = 1024
DFF = 2816
NFC = 22
ALPHA = 4 ** 0.25
LN_EPS = 1e-5
EV_IN = 2560
OD_IN = 2328


class Buf:
    __slots__ = ("last_w", "readers")

    def __init__(self):
        self.last_w = None
        self.readers = []


class Op:
    __slots__ = ("eng", "fn", "deps", "odeps", "is_dma", "sig", "signal", "cost", "idx", "pri")

    def __init__(self, eng, fn, is_dma, cost=300.0):
        self.eng = eng
        self.fn = fn
        self.is_dma = is_dma
        self.deps = []
        self.odeps = []
        self.sig = None
        self.signal = False
        self.cost = cost
        self.idx = 0
        self.pri = 0


class Prog:
    ENGS = ("pe", "act", "dve", "pool", "sp")

    def __init__(self, nc):
        self.nc = nc
        self.es = ExitStack()
        self.pending = []
        self.known = {e: {} for e in self.ENGS}
        self.esem = {}
        self.ecnt = {}
        self.nsem = 0
        for e in self.ENGS:
            self._new_esem(e)
        self.dsem = {}
        self.dcnt = {}
        for q, n in (("sp", 36), ("pool", 56)):
            self.dsem[q] = [self._sem(f"d_{q}_{i}") for i in range(n)]
            self.dcnt[q] = 0
        self.nops = 0
        self.cur_bias = 0
        self.dma_live = []
        self.last_op = {e: None for e in self.ENGS}

    def _sem(self, name):
        self.nsem += 1
        return self.es.enter_context(self.nc.semaphore(f"{name}_{self.nsem}"))

    def _new_esem(self, e):
        self.esem[e] = self._sem(f"e_{e}")
        self.ecnt[e] = 0

    def op(self, eng, fn, reads=(), writes=(), is_dma=False, cost=300.0):
        o = Op(eng, fn, is_dma, cost)
        deps = set()
        for b in reads:
            if b.last_w is not None:
                deps.add(b.last_w)
        for b in writes:
            if b.last_w is not None:
                deps.add(b.last_w)
            for r in b.readers:
                deps.add(r)
        for d in deps:
            o.odeps.append(d)
            if d.eng == "pe" and eng == "pe" and not d.is_dma and not is_dma:
                continue
            o.deps.append(d)
        for b in writes:
            b.last_w = o
            b.readers = []
        for b in reads:
            if b not in writes:
                b.readers.append(o)
        o.pri = self.cur_bias
        self.pending.append(o)
        self.nops += 1
        if is_dma:
            self.dma_live.append(o)
        return o

    def barrier(self):
        self.flush()
        lasts = [o for o in self.last_op.values() if o is not None]
        dmas = list(self.dma_live)
        self.dma_live = []
        for e in self.ENGS:
            o = Op(e, lambda eng: eng.nop(), False, 50.0)
            o.deps = [d for d in lasts if d.eng != e] + dmas
            o.odeps = list(o.deps)
            self.pending.append(o)
        self.flush()

    def dma(self, q, out, in_, reads=(), writes=(), slow=False):
        nbytes = float(np.prod(out.shape)) * 3.0
        return self.op(q, lambda e: e.dma_start(out=out, in_=in_), reads, writes, is_dma=True, cost=nbytes)

    def schedule(self, ops):
        n = len(ops)
        pri = [0] * n
        for i, o in enumerate(ops):
            o.idx = i
            pri[i] = i + o.pri
        inwin = set(ops)
        indeg = [0] * n
        succ = [[] for _ in range(n)]
        for o in ops:
            for d in o.odeps:
                if d in inwin:
                    indeg[o.idx] += 1
                    succ[d.idx].append(o.idx)
        ready_t = [0.0] * n
        ready = {e: [] for e in self.ENGS}
        for o in ops:
            if indeg[o.idx] == 0:
                ready[o.eng].append(o.idx)
        eng_free = {e: 0.0 for e in self.ENGS}
        dma_free = 0.0
        order = {e: [] for e in self.ENGS}
        done = 0
        SYNC = 200.0
        while done < n:
            best = None
            for e in self.ENGS:
                r = ready[e]
                if not r:
                    continue
                T = eng_free[e]
                cand = None
                for i in r:
                    rt = ready_t[i]
                    key = (0.0, pri[i]) if rt <= T else (rt, pri[i])
                    if cand is None or key < cand[0]:
                        cand = (key, i)
                i = cand[1]
                st = max(T, ready_t[i])
                if best is None or (st, pri[i]) < (best[0], pri[best[1]]):
                    best = (st, i, e)
            st, i, e = best
            o = ops[i]
            ready[e].remove(i)
            if o.is_dma:
                eng_free[e] = st + 60.0
                xfer = o.cost / 3.0 / 150.0
                dma_free = max(dma_free, st) + xfer
                fin = dma_free + 1800.0
            else:
                fin = st + o.cost
                eng_free[e] = fin
            order[e].append(o)
            done += 1
            for j in succ[i]:
                s2 = ops[j]
                lat = 0.0 if (s2.eng == e and e == "pe" and not o.is_dma and not s2.is_dma) else SYNC
                t = fin + lat
                if t > ready_t[j]:
                    ready_t[j] = t
                indeg[j] -= 1
                if indeg[j] == 0:
                    ready[s2.eng].append(j)
        return order

    def flush(self):
        ops = self.pending
        self.pending = []
        if not ops:
            return
        pend = set(ops)
        per = self.schedule(ops)
        for e in self.ENGS:
            comp = [o for o in per[e] if not o.is_dma]
            if comp:
                self.last_op[e] = comp[-1]
        for o in ops:
            for d in o.deps:
                if d in pend and not d.is_dma:
                    d.signal = True
        for e in self.ENGS:
            comp = [o for o in per[e] if not o.is_dma]
            if comp:
                comp[-1].signal = True
            i = 0
            n = len(comp)
            while i < n:
                j = i
                while not comp[j].signal:
                    j += 1
                if self.ecnt[e] >= SEM_LIMIT:
                    self._new_esem(e)
                self.ecnt[e] += 1
                sv = (self.esem[e], self.ecnt[e])
                for k in range(i, j + 1):
                    comp[k].sig = sv
                i = j + 1
        prewait = {}
        for o in [x for e in self.ENGS for x in per[e]]:
            if o.is_dma:
                q = o.eng
                n = self.dcnt[q]
                pool = self.dsem[q]
                slot = n % len(pool)
                rnd = n // len(pool)
                o.sig = (pool[slot], 16 * (rnd + 1))
                if rnd > 0:
                    prewait[o] = (pool[slot], 16 * rnd)
                self.dcnt[q] = n + 1
        known = self.known
        with self.nc.Block() as block:
            def run(e):
                lst = per[e]

                def body(engine):
                    kn = known[e]
                    for o in lst:
                        best = {}
                        ws = [d.sig for d in o.deps]
                        if o in prewait:
                            ws.append(prewait[o])
                        for (s, v) in ws:
                            if kn.get(s, 0) >= v:
                                continue
                            if best.get(s, 0) < v:
                                best[s] = v
                        for s, v in best.items():
                            engine.wait_ge(s, v)
                            kn[s] = v
                        ins = o.fn(engine)
                        if o.is_dma:
                            ins.then_inc(o.sig[0], 16)
                        elif o.signal:
                            ins.then_inc(o.sig[0], 1)
                return body
            if per["pe"]:
                block.tensor(run("pe"))
            if per["act"]:
                block.scalar(run("act"))
            if per["dve"]:
                block.vector(run("dve"))
            if per["pool"]:
                block.gpsimd(run("pool"))
            if per["sp"]:
                block.sync(run("sp"))
        for o in ops:
            o.fn = None
            o.deps = None
            o.odeps = None

    def finish(self, out_bufs):
        self.op("sp", lambda e: e.nop(), reads=list(out_bufs), writes=())
        self.flush()
        self.es.close()


class DT:
    def __init__(self, ap):
        self.ap = ap
        self.bufs = {}

    def b(self, key):
        if key not in self.bufs:
            self.bufs[key] = Buf()
        return self.bufs[key]

    def all(self, pred=None):
        return [v for k, v in self.bufs.items() if pred is None or pred(k)]


class Rot:
    def __init__(self, items):
        self.items = items
        self.i = 0

    def next(self):
        it = self.items[self.i % len(self.items)]
        self.i += 1
        return it


def host_consts():
    c = {}
    c["ident"] = np.eye(128, dtype=np.float32)
    j = np.arange(128)[:, None]
    s = np.arange(128)[None, :]
    c["negiu"] = np.where(j >= s, -1.0, 0.0).astype(np.float32)
    c["tril"] = np.where(s >= j, 1.0, 0.0).astype(np.float32)
    sp = np.arange(128)[:, None]
    t = np.arange(512)[None, :]
    sbm = np.zeros((128, 4, 512), np.float32)
    cmi = np.zeros((128, 4, 512), np.float32)
    for rel in range(4):
        sbm[:, rel, :] = (sp + rel * 128 < t)
        cmi[:, rel, :] = (sp + rel * 128 <= t)
    c["sbmask"] = sbm
    c["cmaski"] = cmi
    wm = np.zeros((128, 8, 512), np.float32)
    for rel in range(8):
        sa = (rel - 4) * 128 + sp
        wm[:, rel, :] = (sa <= t) & (t - sa < 512)
    c["wmask"] = wm
    n = np.arange(127)[:, None]
    tt = np.arange(S)[None, :]
    cm = np.zeros((128, S), np.float32)
    cm[:127] = (16 * n + 31 <= tt)
    c["cmask"] = cm
    cstart = np.arange(127) * 16
    sstart = np.arange(32) * 64
    ov = np.zeros((128, 32), np.float32)
    ov[:127] = ((cstart[:, None] < sstart[None, :] + 64) & (cstart[:, None] + 32 > sstart[None, :]))
    c["overlap"] = ov
    blk = np.arange(S) // 64
    jj = np.arange(32)[None, :]
    valid = jj <= blk[:, None]
    forced = (jj == 0) | (jj == blk[:, None]) | (jj == blk[:, None] - 1)
    sa = np.where(valid, np.where(forced, 1e3, 0.0), -1e30).astype(np.float32)
    c["scoreadd"] = np.ascontiguousarray(sa.reshape(16, 128, 32).transpose(1, 0, 2))
    E = np.zeros((32, 16, 128), np.float32)
    for kb in range(16):
        for ss in range(128):
            E[2 * kb + ss // 64, kb, ss] = 1.0
    c["emat"] = E
    NEGB = -30000.0
    en = np.zeros((128, 16, 128), np.float32)
    en[:32] = E * NEGB
    c["ematn"] = en
    c["cmneg"] = np.where(cm > 0, 0.0, NEGB).astype(np.float32)
    tri = np.zeros((128, 2, 128), np.float32)
    tri[:, 0, :] = np.where(j <= s, 0.0, NEGB)
    tri[:, 1, :] = np.where(j > s, 0.0, NEGB)
    c["trineg"] = tri
    return c


CONST_SHAPES = {
    "ident": [128, 128], "negiu": [128, 128], "tril": [128, 128], "sbmask": [128, 4, 512],
    "cmaski": [128, 4, 512], "wmask": [128, 8, 512], "cmask": [128, S], "overlap": [128, 32],
    "scoreadd": [128, 16, 32], "emat": [32, 16, 128], "ematn": [128, 16, 128], "cmneg": [128, S],
    "trineg": [128, 2, 128],
}

WEIGHT_SHAPES = {
    "ev_w_in": [D, EV_IN], "ev_conv_w": [31, 512], "ev_conv_b": [512], "ev_conv_ln_g": [512],
    "ev_conv_ln_b": [512], "ev_w_out": [D, D], "od_w_in": [D, OD_IN], "od_cmpk_pos": [32, 64],
    "od_cmpk_w1": [2048, 64], "od_cmpk_w2": [64, 64], "od_cmpv_pos": [32, 64], "od_cmpv_w1": [2048, 64],
    "od_cmpv_w2": [64, 64], "od_gmlp_ln_g": [512], "od_gmlp_ln_b": [512], "od_gmlp_ws": [4, 128, 128],
    "od_gmlp_bs": [4, 128], "od_w_out": [D, D], "ffn_w_gate": [2, D, DFF], "ffn_w_up": [2, D, DFF],
    "ffn_w_down": [2, DFF, D], "ln1_g": [2, D], "ln1_b": [2, D], "ln2_g": [2, D], "ln2_b": [2, D],
}


def build(ntok=4096, debug_outs=(), stages=None):
    nseq = ntok // S
    NT = ntok // 128
    nc = bass.Bass("TRN2", target_bir_lowering=False)
    P = Prog(nc)
    IN = {}
    IN["x"] = nc.dram_tensor("x", [ntok, D], F32, kind="ExternalInput").ap()
    for k, shp in WEIGHT_SHAPES.items():
        IN[k] = nc.dram_tensor(k, shp, F32, kind="ExternalInput").ap()
    for k, shp in CONST_SHAPES.items():
        IN[k] = nc.dram_tensor("c_" + k, shp, F32, kind="ExternalInput").ap()
    out_dt = DT(nc.dram_tensor("out", [ntok, D], F32, kind="ExternalOutput").ap())

    pooled = len(debug_outs) == 0
    POOL_ELEMS = 12613632 + 4096
    pool_state = {"L0": 0, "L1": 0}
    if pooled:
        pool_ap = nc.dram_tensor("scr_pool", [POOL_ELEMS], BF16).ap()

    def scratch(name, shape, dt, grp=None):
        if pooled and grp is not None:
            n = int(np.prod(shape))
            off = pool_state[grp]
            pool_state[grp] = off + n
            assert pool_state[grp] <= POOL_ELEMS
            letters = "abcd"[:len(shape)]
            pat = "(" + " ".join(letters) + ") -> " + " ".join(letters)
            kw = {letters[i]: shape[i] for i in range(len(shape))}
            return DT(pool_ap[off:off + n].rearrange(pat, **kw))
        kind = "ExternalOutput" if name in debug_outs else "Internal"
        return DT(nc.dram_tensor(name, shape, dt, kind=kind).ap())

    SC = {}
    SC["qT0"] = scratch("qT0", [nseq, 4, 128, S], BF16, "L0")
    SC["kT0"] = scratch("kT0", [nseq, 4, 128, S], BF16, "L0")
    SC["v0"] = scratch("v0", [ntok, 512], BF16, "L0")
    SC["hT0"] = scratch("hT0", [nseq, 4, 128, S + 30], BF16, "L0")
    SC["osbT"] = scratch("osbT", [nseq, 4, 128, S], BF16, "L0")
    SC["ocvT"] = scratch("ocvT", [nseq, 4, 128, S], BF16, "L0")
    SC["x1"] = scratch("x1", [ntok, D], F32)
    if pooled:
        SC["x2"] = out_dt
        SC["x3"] = SC["x1"]
    else:
        SC["x2"] = scratch("x2", [ntok, D], F32)
        SC["x3"] = scratch("x3", [ntok, D], F32)
    SC["q1T"] = scratch("q1T", [nseq, 4, 128, S], BF16, "L1")
    for nm in ("kcT", "vcT", "ksT", "kwT"):
        SC[nm] = scratch(nm, [nseq, 128, S], BF16, "L1")
    SC["vsA"] = scratch("vsA", [ntok, 130], BF16, "L1")
    SC["vwA"] = scratch("vwA", [ntok, 130], BF16, "L1")
    SC["gates"] = scratch("gates", [ntok, 24], F32)
    SC["omlpT"] = scratch("omlpT", [nseq, 4, 128, S], BF16, "L1")
    SC["onsaT"] = scratch("onsaT", [nseq, 4, 128, S], BF16, "L1")

    top = ExitStack()

    _cnt = [0]

    def alloc(es, name, shape, dt=F32):
        _cnt[0] += 1
        return es.enter_context(nc.sbuf_tensor(f"sb{_cnt[0]}_{name}", shape, dt))

    PSB = []
    PSBIG = []
    for i in range(4):
        t = top.enter_context(nc.psum_tensor(f"psbig{i}", [128, 1024], F32))
        PSBIG.append(t)
        PSB.append((t[:, 0:512], Buf()))
        PSB.append((t[:, 512:1024], Buf()))

    ident = alloc(top, "ident", [128, 128]); b_ident = Buf()
    identb = alloc(top, "identb", [128, 128], BF16); b_identb = Buf()
    P.dma("sp", ident[:], IN["ident"], writes=[b_ident])
    P.dma("pool", identb[:], IN["ident"], writes=[b_identb])

    def fsize(ap):
        shp = ap.shape
        n = 1
        for v in shp[1:]:
            n *= int(v)
        return n

    def MM(out, lhsT, rhs, st, sp_, R, W, sgc=False):
        c = max(fsize(rhs), 64) / 2.4 + 8.0
        if lhsT.dtype == F32:
            c *= 4.0
        if sgc:
            P.op("pe", lambda e: e.matmul(out, lhsT, rhs, start=st, stop=sp_, skip_group_check=True), R, W, cost=c)
        else:
            P.op("pe", lambda e: e.matmul(out, lhsT, rhs, start=st, stop=sp_), R, W, cost=c)

    def TR(out, in_, idn, R, W):
        P.op("pe", lambda e: e.transpose(out, in_, idn), R + [b_ident], W, cost=230.0)

    def ACT(out, in_, func, R, W, bias=None, scale=None):
        kw = {}
        if bias is not None:
            kw["bias"] = bias
        if scale is not None:
            kw["scale"] = scale
        P.op("act", lambda e: e.activation(out=out, in_=in_, func=func, **kw), R, W, cost=200.0 + fsize(out) / 1.4)

    def V(eng, meth, R, W, **kw):
        o = kw.get("out", kw.get("ap"))
        f = fsize(kw["in_"]) if meth in ("bn_stats", "max") else fsize(o)
        c = (70.0 + f / 0.96) if eng == "dve" else (150.0 + f / 0.7)
        P.op(eng, lambda e: getattr(e, meth)(**kw), R, W, cost=c)

    def evac(eng, out, in_, R, W, scale=None):
        if eng == "act":
            if scale is None:
                ACT(out, in_, AF.Identity, R, W)
            else:
                ACT(out, in_, AF.Identity, R, W, scale=scale)
        else:
            if scale is None:
                V("dve", "tensor_copy", R, W, out=out, in_=in_)
            else:
                V("dve", "tensor_scalar", R, W, out=out, in0=in_, scalar1=scale, scalar2=None, op0=ALU.mult)

    def load_xT(src, r0, ntile, xs, bxs, xT, bxT, psrot):
        n = 128 * ntile
        P.dma("sp", xs[:, 0:ntile, :], src.ap[r0:r0 + n, :].rearrange("(j p) f -> p j f", p=128),
              reads=src.all(lambda k: k[0] <= r0 < k[1] or r0 <= k[0] < r0 + n), writes=[bxs])
        per_bank = 512 // n if n < 512 else 1
        kc = 0
        tog = 0
        while kc < 8:
            pt, pb = psrot.next()
            nk = min(per_bank, 8 - kc)
            for q in range(nk):
                for j in range(ntile):
                    TR(pt[:, q * n + j * 128: q * n + (j + 1) * 128], xs[:, j, (kc + q) * 128:(kc + q + 1) * 128],
                       ident[:], [bxs], [pb])
            evac("act" if tog else "dve", xT[:, kc:kc + nk, :].rearrange("p k t -> p (k t)"), pt[:, 0:nk * n], [pb], [bxT])
            tog ^= 1
            kc += nk

    def load_w_kc(es, name, src_ap, ncol, splits=None):
        w = alloc(es, name, [128, 8, ncol], BF16)
        if splits is None:
            b = Buf()
            for kc in range(8):
                P.dma("pool", w[:, kc, :], src_ap[kc * 128:(kc + 1) * 128, :], writes=[b])
            return w, b
        bufs = []
        for gi in range(len(splits) - 1):
            c0, c1 = splits[gi], splits[gi + 1]
            b = Buf()
            for kc in range(8):
                P.dma("pool", w[:, kc, c0:c1], src_ap[kc * 128:(kc + 1) * 128, c0:c1], writes=[b])
            bufs.append(b)
        return w, bufs

    I32 = mybir.dt.int32

    def rsqrt_small(sd, rstd, tmp, bsm):
        V("dve", "tensor_scalar", [bsm], [bsm], out=tmp[:, 0:1].bitcast(I32), in0=sd[:, 0:1].bitcast(I32), scalar1=1, scalar2=None,
          op0=ALU.logical_shift_right)
        V("dve", "tensor_scalar", [bsm], [bsm], out=rstd[:, 0:1].bitcast(I32), in0=tmp[:, 0:1].bitcast(I32), scalar1=-1.0,
          scalar2=float(0x5f3759df), op0=ALU.mult, op1=ALU.add)
        for _ in range(3):
            V("dve", "tensor_tensor", [bsm], [bsm], out=tmp[:, 1:2], in0=sd[:, 0:1], in1=rstd[:, 0:1], op=ALU.mult)
            V("dve", "scalar_tensor_tensor", [bsm], [bsm], out=tmp[:, 2:3], in0=tmp[:, 1:2], scalar=-0.5, in1=rstd[:, 0:1],
              op0=ALU.mult, op1=ALU.mult)
            V("dve", "scalar_tensor_tensor", [bsm], [bsm], out=rstd[:, 0:1], in0=tmp[:, 2:3], scalar=1.5, in1=rstd[:, 0:1],
              op0=ALU.add, op1=ALU.mult)

    def ln_tail(es_tmp, y, by, gbc, bbc, bgb, out, bout, small, gm_eng="pool"):
        st, mv, sd, rstd, nmr, tmp, bsm = small
        V("dve", "bn_stats", [by], [bsm], out=st[:, 0:6], in_=y[:, 0:512])
        V("dve", "bn_stats", [by, bsm], [bsm], out=st[:, 6:12], in_=y[:, 512:1024])
        V("dve", "bn_aggr", [bsm], [bsm], out=mv[:], in_=st[:, 0:12])
        ACT(sd[:], mv[:, 1:2], AF.Sqrt, [bsm], [bsm], bias=LN_EPS)
        V("dve", "reciprocal", [bsm], [bsm], out=rstd[:], in_=sd[:])
        V("dve", "tensor_scalar", [bsm], [bsm], out=nmr[:], in0=mv[:, 0:1], scalar1=rstd[:, 0:1], scalar2=-1.0,
          op0=ALU.mult, op1=ALU.mult)
        ACT(y[:], y[:], AF.Identity, [by, bsm], [by], bias=nmr[:, 0:1], scale=rstd[:, 0:1])
        V(gm_eng, "tensor_tensor", [by, bgb], [by], out=y[:], in0=y[:], in1=gbc[:], op=ALU.mult)
        V("dve", "tensor_tensor", [by, bgb], [bout], out=out[:], in0=y[:], in1=bbc[:], op=ALU.add)

    def small_ln(es, tag):
        st = alloc(es, f"st{tag}", [128, 12]); mv = alloc(es, f"mv{tag}", [128, 2])
        sd = alloc(es, f"sd{tag}", [128, 1]); rstd = alloc(es, f"rs{tag}", [128, 1]); nmr = alloc(es, f"nm{tag}", [128, 1])
        tmp = alloc(es, f"tm{tag}", [128, 4])
        return (st, mv, sd, rstd, nmr, tmp, Buf())

    def want(name):
        return stages is None or name in stages

    def stage_l0_inproj():
        es = ExitStack()
        W, bWs = load_w_kc(es, "w0in", IN["ev_w_in"], EV_IN, [0, 512, 1024, 1536, 2560])
        bWq, bWk, bWv, bWa = bWs
        xs_r = Rot([(alloc(es, f"xs{i}", [128, 4, D]), Buf()) for i in range(2)])
        xT_r = Rot([(alloc(es, f"xT{i}", [128, 8, 512], BF16), Buf()) for i in range(2)])
        stg_r = Rot([(alloc(es, f"stg{i}", [128, 512], BF16), Buf()) for i in range(6)])
        sig_r = Rot([(alloc(es, f"sig{i}", [128, 512]), Buf()) for i in range(2)])
        zt = alloc(es, "zt", [128, 32], BF16); bz = Buf()
        V("pool", "memset", [], [bz], ap=zt[:], constant=0.0)
        for sq in range(nseq):
            for cc in range(4):
                P.dma("sp", SC["hT0"].ap[sq, cc, :, 0:30], zt[:, 0:30], reads=[bz], writes=[SC["hT0"].b((sq, cc, -1))])
        psT = Rot(PSB[0:2])
        psM = Rot(PSB[2:8])
        tog = 0
        for tc in range(ntok // 512):
            sq = tc // 4
            t0 = (tc % 4) * 512
            xs, bxs = xs_r.next()
            xT, bxT = xT_r.next()
            load_xT(DT(IN["x"]), tc * 512, 4, xs, bxs, xT, bxT, psT)
            for which, col0, dst, scale in (("q", 0, "qT0", 0.125), ("k", 512, "kT0", None)):
                for hp in range(4):
                    pt, pb = psM.next()
                    for kc in range(8):
                        MM(pt[:, :], W[:, kc, col0 + hp * 128: col0 + (hp + 1) * 128], xT[:, kc, :], kc == 0, kc == 7,
                           [bWq if which == "q" else bWk, bxT], [pb])
                    sg, bsg = stg_r.next()
                    evac("act" if tog else "dve", sg[:], pt[:, :], [pb], [bsg], scale=scale)
                    tog ^= 1
                    P.dma("sp", SC[dst].ap[sq, hp, :, t0:t0 + 512], sg[:], reads=[bsg], writes=[SC[dst].b((sq, hp, tc))])
            for cc in range(4):
                pa, pba = psM.next()
                pg, pbg = psM.next()
                for kc in range(8):
                    MM(pa[:, :], W[:, kc, 1536 + cc * 128: 1536 + (cc + 1) * 128], xT[:, kc, :], kc == 0, kc == 7, [bWa, bxT], [pba])
                for kc in range(8):
                    MM(pg[:, :], W[:, kc, 2048 + cc * 128: 2048 + (cc + 1) * 128], xT[:, kc, :], kc == 0, kc == 7, [bWa, bxT], [pbg])
                sgm, bsgm = sig_r.next()
                ACT(sgm[:], pg[:, :], AF.Sigmoid, [pbg], [bsgm])
                sg, bsg = stg_r.next()
                V("dve", "tensor_tensor", [pba, bsgm], [bsg], out=sg[:], in0=pa[:, :], in1=sgm[:], op=ALU.mult)
                P.dma("sp", SC["hT0"].ap[sq, cc, :, 30 + t0:30 + t0 + 512], sg[:], reads=[bsg], writes=[SC["hT0"].b((sq, cc, tc))])
            for j in range(4):
                pt, pb = psM.next()
                for kc in range(8):
                    MM(pt[:, :], xT[:, kc, j * 128:(j + 1) * 128], W[:, kc, 1024:1536], kc == 0, kc == 7, [bWv, bxT], [pb])
                sg, bsg = stg_r.next()
                evac("act" if tog else "dve", sg[:], pt[:, :], [pb], [bsg])
                tog ^= 1
                r0 = tc * 512 + j * 128
                P.dma("sp", SC["v0"].ap[r0:r0 + 128, :], sg[:], reads=[bsg], writes=[SC["v0"].b((sq, r0))])
        P.barrier()
        es.close()

    def stage_l0_sb(fuse_conv=False):
        es = ExitStack()
        negiu = alloc(es, "negiu", [128, 128], BF16); negones = alloc(es, "negones", [128, 128], BF16)
        sbm = alloc(es, "sbm", [128, 128], BF16); bc = Buf()
        P.dma("pool", negiu[:], IN["negiu"], writes=[bc])
        P.dma("pool", sbm[:], IN["sbmask"][:, 0, 0:128], writes=[bc])
        V("pool", "memset", [], [bc], ap=negones[:], constant=-1.0)
        qk_l = []
        for i in range(2):
            qz = alloc(es, f"sbq{i}", [128, 2, S], BF16); kk_ = alloc(es, f"sbk{i}", [128, S], BF16)
            vz = alloc(es, f"sbv{i}", [128, 16, 2, 128], BF16); bb = Buf()
            V("pool", "memset", [], [bb], ap=qz[:], constant=0.0)
            V("pool", "memset", [], [bb], ap=vz[:], constant=0.0)
            qk_l.append((qz, kk_, vz, bb))
        qk_r = Rot(qk_l)
        e1_r = Rot([(alloc(es, f"e1_{i}", [128, 2, 512]), Buf()) for i in range(3)])
        sp_r = Rot([(alloc(es, f"sp_{i}", [128, 2, 512], BF16), Buf()) for i in range(4)])
        w_r = Rot([(alloc(es, f"w_{i}", [128, 2, 512], BF16), Buf()) for i in range(3)])
        S_r = Rot([(alloc(es, f"S_{i}", [128, 2, 512], BF16), Buf()) for i in range(3)])
        o_r = Rot([(alloc(es, f"osg{i}", [128, 512], BF16), Buf()) for i in range(3)])
        psA = Rot([(PSBIG[i], [PSB[2 * i][1], PSB[2 * i + 1][1]]) for i in range(3)])
        psO = Rot([PSB[6]] if fuse_conv else [PSB[6], PSB[7]])
        for sq in range(nseq):
            for hp in range(4):
                q, k, v, bqk = qk_r.next()
                for e in range(2):
                    P.dma("sp", q[64 * e:64 * e + 64, e, :], SC["qT0"].ap[sq, hp, 64 * e:64 * e + 64, :],
                          reads=SC["qT0"].all(lambda kk: kk[0] == sq and kk[1] == hp), writes=[bqk])
                    P.dma("sp", v[:, :, e, 64 * e:64 * e + 64],
                          SC["v0"].ap[sq * S:(sq + 1) * S, hp * 128 + 64 * e:hp * 128 + 64 * e + 64].rearrange("(j p) d -> p j d", p=128),
                          reads=SC["v0"].all(lambda kk: kk[0] == sq), writes=[bqk])
                P.dma("sp", k[:], SC["kT0"].ap[sq, hp], reads=SC["kT0"].all(lambda kk: kk[0] == sq and kk[1] == hp), writes=[bqk])
                for qc in range(4):
                    t0 = qc * 512
                    kbs = list(range(4 * qc + 3, -1, -1))
                    Sprev = None
                    pot, pob = psO.next()
                    for idx, kb in enumerate(kbs):
                        first = idx == 0
                        last = kb == 0
                        diag = kb >= 4 * qc
                        c0 = (kb - 4 * qc) * 128 if diag else 0
                        c1 = c0 + 128
                        s0 = c1 if diag else 0
                        pa, pba = psA.next()
                        pa3 = pa[:, :].rearrange("p (e t) -> p e t", e=2)
                        for e in range(2):
                            MM(pa[:, e * 512 + c0:(e + 1) * 512], k[:, kb * 128:(kb + 1) * 128], q[:, e, t0 + c0:t0 + 512], True, True, [bqk], pba)
                        e1, be1 = e1_r.next()
                        ACT(e1[:, :, c0:512], pa3[:, :, c0:512], AF.Exp, pba, [be1])
                        spt, bsp = sp_r.next()
                        ACT(spt[:, :, c0:512], e1[:, :, c0:512], AF.Ln, [be1], [bsp], bias=1.0)
                        if diag:
                            for e in range(2):
                                V("pool", "tensor_tensor", [bsp, bc], [bsp], out=spt[:, e, c0:c1], in0=spt[:, e, c0:c1], in1=sbm[:], op=ALU.mult)
                        for e in range(2):
                            MM(pa[:, e * 512 + c0:(e + 1) * 512], negiu[:], spt[:, e, c0:512], False, True, [bc, bsp], pba, sgc=True)
                            if not first and s0 < 512:
                                Sp, bSp = Sprev
                                MM(pa[:, e * 512 + s0:(e + 1) * 512], negones[:], Sp[:, e, s0:512], False, True, [bc, bSp], pba, sgc=True)
                        wt, bw = w_r.next()
                        ACT(wt[:, :, c0:512], pa3[:, :, c0:512], AF.Exp, pba, [bw])
                        if diag:
                            for e in range(2):
                                V("pool", "tensor_tensor", [bw, bc], [bw], out=wt[:, e, c0:c1], in0=wt[:, e, c0:c1], in1=sbm[:], op=ALU.mult)
                        for e in range(2):
                            MM(pot[:, c0:512], v[:, kb, e, :], wt[:, e, c0:512], first and e == 0, last and e == 1, [bqk, bw], [pob], sgc=True)
                        if not last:
                            Sn, bSn = S_r.next()
                            if diag:
                                V("dve", "tensor_copy", [bsp], [bSn], out=Sn[:, :, c0:c1], in_=spt[:, :, c0:c1])
                                if not first:
                                    Sp, bSp = Sprev
                                    V("dve", "tensor_tensor", [bsp, bSp, bSn], [bSn], out=Sn[:, :, c1:512], in0=Sp[:, :, c1:512], in1=spt[:, :, c1:512], op=ALU.add)
                            else:
                                Sp, bSp = Sprev
                                V("dve", "tensor_tensor", [bsp, bSp], [bSn], out=Sn[:], in0=Sp[:], in1=spt[:], op=ALU.add)
                            Sprev = (Sn, bSn)
                    og, bog = o_r.next()
                    evac("dve", og[:], pot[:, :], [pob], [bog])
                    P.dma("sp", SC["osbT"].ap[sq, hp, :, t0:t0 + 512], og[:], reads=[bog], writes=[SC["osbT"].b((sq, hp, qc))])
        if fuse_conv:
            P.cur_bias = 1000000
            conv_emit(es, PSB[7], PSB[7], True)
            P.cur_bias = 0
        P.barrier()
        es.close()

    def conv_emit(es, psYb, psSb, exp_only):
        bc = Buf()
        cw = alloc(es, "cw", [32, 512]); vec = alloc(es, "cvec", [12, 128])
        P.dma("sp", cw[0:31, :], IN["ev_conv_w"], writes=[bc])
        for i, nm in enumerate(("ev_conv_b", "ev_conv_ln_g", "ev_conv_ln_b")):
            P.dma("sp", vec[4 * i:4 * i + 4, :], IN[nm].rearrange("(c p) -> c p", p=128), writes=[bc])
        cwT = alloc(es, "cwT", [128, 4 * 31]); vecT = alloc(es, "cvecT", [128, 12]); nvecT = alloc(es, "cnvecT", [128, 12]); bct = Buf()
        pt, pb = psYb
        for cc in range(4):
            TR(pt[:, cc * 31:(cc + 1) * 31], cw[0:31, cc * 128:(cc + 1) * 128], ident[0:31, 0:31], [bc], [pb])
        TR(pt[:, 128:140], vec[0:12, :], ident[0:12, 0:12], [bc], [pb])
        V("dve", "tensor_copy", [pb], [bct], out=cwT[:], in_=pt[:, 0:124])
        V("dve", "tensor_copy", [pb], [bct], out=vecT[:], in_=pt[:, 128:140])
        V("dve", "tensor_scalar", [bct], [bct], out=nvecT[:], in0=vecT[:], scalar1=-1.0, scalar2=None, op0=ALU.mult)
        Dg = alloc(es, "Dg", [128, 4, 31, 128], BF16); bD = [Buf() for _ in range(4)]
        for cc in range(4):
            for w in range(31):
                V("dve", "tensor_scalar", [bct, b_identb], [bD[cc]], out=Dg[:, cc, w, :], in0=identb[:],
                  scalar1=cwT[:, cc * 31 + w: cc * 31 + w + 1], scalar2=None, op0=ALU.mult)
        onesm = alloc(es, "onesm", [128, 128]); bon = Buf()
        V("pool", "memset", [], [bon], ap=onesm[:], constant=1.0 / 512.0)
        hT_r = Rot([(alloc(es, f"hT{i}", [128, 4, S + 30], BF16), Buf()) for i in range(1)])
        y_r = Rot([(alloc(es, f"cy{i}", [128, 4, 512]), Buf()) for i in range(2)])
        ysq_r = Rot([(alloc(es, f"cys{i}", [128, 4, 512]), Buf()) for i in range(2)])
        mean_r = Rot([(alloc(es, f"cmean{i}", [128, 512]), alloc(es, f"crstd{i}", [128, 512]), Buf()) for i in range(2)])
        yn_r = Rot([(alloc(es, f"cyn{i}", [128, 512]), alloc(es, f"cxg{i}", [128, 512]), alloc(es, f"cee{i}", [128, 512]), Buf()) for i in range(2)])
        og_r = Rot([(alloc(es, f"cog{i}", [128, 512], BF16), Buf()) for i in range(3)])
        for sq in range(nseq):
            hT, bh = hT_r.next()
            for cc in range(4):
                P.dma("sp", hT[:, cc, :], SC["hT0"].ap[sq, cc], reads=SC["hT0"].all(lambda kk: kk[0] == sq and kk[1] == cc), writes=[bh])
            for tq in range(4):
                t0 = tq * 512
                y, by = y_r.next()
                ysq, bysq = ysq_r.next()
                for cc in range(4):
                    pt, pb = psYb
                    for w in range(31):
                        MM(pt[:, :], Dg[:, cc, w, :], hT[:, cc, t0 + w:t0 + w + 512], w == 0, w == 30, [bD[cc], bh], [pb])
                    V("dve", "tensor_scalar", [pb, bct], [by], out=y[:, cc, :], in0=pt[:, :], scalar1=vecT[:, cc:cc + 1], scalar2=None, op0=ALU.add)
                    V("pool", "tensor_tensor", [by], [bysq], out=ysq[:, cc, :], in0=y[:, cc, :], in1=y[:, cc, :], op=ALU.mult)
                pm, pbm = psSb
                for cc in range(4):
                    MM(pm[:, :], onesm[:], y[:, cc, :], cc == 0, cc == 3, [bon, by], [pbm])
                mean, rstd, bmr = mean_r.next()
                ACT(mean[:], pm[:, :], AF.Identity, [pbm], [bmr])
                for cc in range(4):
                    MM(pm[:, :], onesm[:], ysq[:, cc, :], cc == 0, cc == 3, [bon, bysq], [pbm])
                V("dve", "tensor_tensor", [bmr], [bmr], out=rstd[:], in0=mean[:], in1=mean[:], op=ALU.mult)
                V("dve", "tensor_tensor", [bmr, pbm], [bmr], out=rstd[:], in0=pm[:, :], in1=rstd[:], op=ALU.subtract)
                if exp_only:
                    ACT(rstd[:], rstd[:], AF.Ln, [bmr], [bmr], bias=LN_EPS)
                    ACT(rstd[:], rstd[:], AF.Exp, [bmr], [bmr], scale=-0.5)
                else:
                    ACT(rstd[:], rstd[:], AF.Sqrt, [bmr], [bmr], bias=LN_EPS)
                    V("dve", "reciprocal", [bmr], [bmr], out=rstd[:], in_=rstd[:])
                for cc in range(4):
                    yn, xg, ee, byn = yn_r.next()
                    V("dve", "tensor_tensor", [by, bmr], [byn], out=yn[:], in0=y[:, cc, :], in1=mean[:], op=ALU.subtract)
                    V("pool", "tensor_tensor", [byn, bmr], [byn], out=yn[:], in0=yn[:], in1=rstd[:], op=ALU.mult)
                    og, bog = og_r.next()
                    if exp_only:
                        ACT(ee[:], yn[:], AF.Exp, [byn, bct], [byn], bias=nvecT[:, 8 + cc:9 + cc], scale=nvecT[:, 4 + cc:5 + cc])
                        V("dve", "tensor_scalar", [byn, bct], [byn], out=xg[:], in0=yn[:], scalar1=vecT[:, 4 + cc:5 + cc],
                          scalar2=vecT[:, 8 + cc:9 + cc], op0=ALU.mult, op1=ALU.add)
                        ACT(ee[:], ee[:], AF.Ln, [byn], [byn], bias=1.0)
                        ACT(ee[:], ee[:], AF.Exp, [byn], [byn], scale=-1.0)
                        V("dve", "tensor_tensor", [byn], [bog], out=og[:], in0=xg[:], in1=ee[:], op=ALU.mult)
                    else:
                        ACT(og[:], yn[:], AF.Silu, [byn, bct], [bog], bias=vecT[:, 8 + cc:9 + cc], scale=vecT[:, 4 + cc:5 + cc])
                    P.dma("sp", SC["ocvT"].ap[sq, cc, :, t0:t0 + 512], og[:], reads=[bog], writes=[SC["ocvT"].b((sq, cc, tq))])

    def stage_l0_conv():
        es = ExitStack()
        conv_emit(es, PSB[1], PSB[2], False)
        P.barrier()
        es.close()

    def stage_outproj(layer, srcA, nA, kA, srcB, wname, xsrc, dst, gname, bname):
        es_ffn = ExitStack()
        Wg_ = alloc(es_ffn, "wg", [128, 8, DFF], BF16); Wu_ = alloc(es_ffn, "wu", [128, 8, DFF], BF16)
        bWg_, bWu_ = Buf(), Buf()
        es = ExitStack()
        bW = Buf()
        WA = alloc(es, "woA", [kA, nA, D], BF16)
        WB = alloc(es, "woB", [128, 4, D], BF16)
        P.dma("pool", WA[:], IN[wname][0:512, :].rearrange("(h d) n -> d h n", d=kA), writes=[bW])
        P.dma("pool", WB[:], IN[wname][512:1024, :].rearrange("(c p) n -> p c n", p=128), writes=[bW])
        P.cur_bias = 300
        for kc in range(8):
            P.dma("pool", Wg_[:, kc, :], IN["ffn_w_gate"][layer][kc * 128:(kc + 1) * 128, :], writes=[bWg_])
            P.dma("pool", Wu_[:, kc, :], IN["ffn_w_up"][layer][kc * 128:(kc + 1) * 128, :], writes=[bWu_])
        P.cur_bias = 0
        pre = (es_ffn, Wg_, bWg_, Wu_, bWu_)
        gbc = alloc(es, "gbc", [128, D]); bbc = alloc(es, "bbc", [128, D]); bgb = Buf()
        P.dma("sp", gbc[:], IN[gname][layer:layer + 1, :].broadcast_to([128, D]), writes=[bgb])
        P.dma("sp", bbc[:], IN[bname][layer:layer + 1, :].broadcast_to([128, D]), writes=[bgb])
        A_r = Rot([(alloc(es, f"lA{i}", [kA, nA, 512], BF16), alloc(es, f"lB{i}", [128, 4, 512], BF16),
                    alloc(es, f"lx{i}", [128, 4, D]), Buf()) for i in range(2)])
        y_r = Rot([(alloc(es, f"ly{i}", [128, D]), Buf()) for i in range(2)])
        o_r = Rot([(alloc(es, f"lo{i}", [128, D]), Buf()) for i in range(2)])
        sm_r = Rot([small_ln(es, i) for i in range(2)])
        psr = Rot([(PSB[0], PSB[1]), (PSB[2], PSB[3]), (PSB[4], PSB[5]), (PSB[6], PSB[7])])
        for tc in range(ntok // 512):
            sq = tc // 4
            t0 = (tc % 4) * 512
            A, B, xs, bl = A_r.next()
            P.dma("sp", A[:], srcA.ap[sq, :, :, t0:t0 + 512].rearrange("h d t -> d h t"),
                  reads=srcA.all(lambda kk: kk[0] == sq), writes=[bl])
            P.dma("sp", B[:], srcB.ap[sq, :, :, t0:t0 + 512].rearrange("c p t -> p c t"),
                  reads=srcB.all(lambda kk: kk[0] == sq), writes=[bl])
            r0 = tc * 512
            P.dma("sp", xs[:], xsrc.ap[r0:r0 + 512, :].rearrange("(j p) f -> p j f", p=128),
                  reads=xsrc.all(lambda kk: r0 <= kk[0] < r0 + 512), writes=[bl])
            for j in range(4):
                banks = psr.next()
                y, by = y_r.next()
                for nh in range(2):
                    pt, pb = banks[nh]
                    n_mm = nA + 4
                    i = 0
                    for h in range(nA):
                        MM(pt[:, :], A[:, h, j * 128:(j + 1) * 128], WA[:, h, nh * 512:(nh + 1) * 512], i == 0, i == n_mm - 1, [bl, bW], [pb])
                        i += 1
                    for c in range(4):
                        MM(pt[:, :], B[:, c, j * 128:(j + 1) * 128], WB[:, c, nh * 512:(nh + 1) * 512], i == 0, i == n_mm - 1, [bl, bW], [pb])
                        i += 1
                    V("dve", "scalar_tensor_tensor", [bl, pb], [by], out=y[:, nh * 512:(nh + 1) * 512], in0=xs[:, j, nh * 512:(nh + 1) * 512],
                      scalar=ALPHA, in1=pt[:, :], op0=ALU.mult, op1=ALU.add)
                o, bo = o_r.next()
                ln_tail(es, y, by, gbc, bbc, bgb, o, bo, sm_r.next(), gm_eng="dve")
                rr = r0 + j * 128
                P.dma("sp", dst.ap[rr:rr + 128, :], o[:], reads=[bo], writes=[dst.b((rr, rr + 128))])
        P.barrier()
        es.close()
        return pre

    def stage_ffn(layer, xsrc, dst, pre=None):
        if pre is None:
            es_ffn = ExitStack()
            Wg, bWg = load_w_kc(es_ffn, "wg", IN["ffn_w_gate"][layer], DFF, None)
            Wu, bWu = load_w_kc(es_ffn, "wu", IN["ffn_w_up"][layer], DFF, None)
        else:
            es_ffn, Wg, bWg, Wu, bWu = pre
        es = ExitStack()
        Wd = alloc(es, "wd", [128, NFC, D], BF16); bWd = Buf()
        for i in range(2):
            P.dma("pool", Wd[:, 11 * i:11 * i + 11, :], IN["ffn_w_down"][layer, 11 * i * 128:(11 * i + 11) * 128, :].rearrange("(c p) n -> p c n", p=128),
                  writes=[bWd])
        gbc = alloc(es, "fgbc", [128, D]); bbc = alloc(es, "fbbc", [128, D]); bgb = Buf()
        P.dma("sp", gbc[:], IN["ln2_g"][layer:layer + 1, :].broadcast_to([128, D]), writes=[bgb])
        P.dma("sp", bbc[:], IN["ln2_b"][layer:layer + 1, :].broadcast_to([128, D]), writes=[bgb])
        xs_r = Rot([(alloc(es, f"fx{i}", [128, 2, D]), Buf()) for i in range(2)])
        xT_r = Rot([(alloc(es, f"fxT{i}", [128, 8, 256], BF16), Buf()) for i in range(2)])
        hT = alloc(es, "fh", [128, NFC, 256], BF16); bh = Buf()
        s_r = Rot([(alloc(es, f"fs{i}", [128, 256]), Buf()) for i in range(3)])
        y_r = Rot([(alloc(es, f"fy{i}", [128, D]), Buf()) for i in range(2)])
        o_r = Rot([(alloc(es, f"fo{i}", [128, D]), Buf()) for i in range(1)])
        sm_r = Rot([small_ln(es, f"f{i}") for i in range(2)])
        psG = Rot(PSB[0:4])
        for tc in range(ntok // 256):
            r0 = tc * 256
            xs, bxs = xs_r.next()
            xT, bxT = xT_r.next()
            load_xT(xsrc, r0, 2, xs, bxs, xT, bxT, psG)
            for fc in range(NFC):
                pg, pbg = psG.next()
                pu, pbu = psG.next()
                for kc in range(8):
                    MM(pg[:, 0:256], Wg[:, kc, fc * 128:(fc + 1) * 128], xT[:, kc, :], kc == 0, kc == 7, [bWg, bxT], [pbg])
                for kc in range(8):
                    MM(pu[:, 0:256], Wu[:, kc, fc * 128:(fc + 1) * 128], xT[:, kc, :], kc == 0, kc == 7, [bWu, bxT], [pbu])
                st, bst = s_r.next()
                ACT(st[:], pg[:, 0:256], AF.Silu, [pbg], [bst])
                V("dve", "tensor_tensor", [bst, pbu], [bh], out=hT[:, fc, :], in0=pu[:, 0:256], in1=st[:], op=ALU.mult)
            for j in range(2):
                y, by = y_r.next()
                for nh in range(2):
                    pt, pb = PSB[4 + 2 * j + nh]
                    for fc in range(NFC):
                        MM(pt[:, :], hT[:, fc, j * 128:(j + 1) * 128], Wd[:, fc, nh * 512:(nh + 1) * 512], fc == 0, fc == NFC - 1, [bh, bWd], [pb])
                    V("dve", "scalar_tensor_tensor", [bxs, pb], [by], out=y[:, nh * 512:(nh + 1) * 512], in0=xs[:, j, nh * 512:(nh + 1) * 512],
                      scalar=ALPHA, in1=pt[:, :], op0=ALU.mult, op1=ALU.add)
                o, bo = o_r.next()
                ln_tail(es, y, by, gbc, bbc, bgb, o, bo, sm_r.next())
                rr = r0 + j * 128
                P.dma("sp", dst.ap[rr:rr + 128, :], o[:], reads=[bo], writes=[dst.b((rr, rr + 128))])
        P.barrier()
        es.close()
        es_ffn.close()

    def stage_l1_inproj(xsrc):
        es = ExitStack()
        W, bWs = load_w_kc(es, "w1in", IN["od_w_in"], OD_IN, [0, 512, 1304, 1816, 2328])
        bWq, bWkv, bWu_, bWv_ = bWs
        bc = Buf()
        wsf = alloc(es, "wsf", [128, 4, 128]); tril = alloc(es, "tril", [128, 128])
        P.dma("sp", wsf[:], IN["od_gmlp_ws"].rearrange("g t s -> t g s"), writes=[bc])
        P.dma("sp", tril[:], IN["tril"], writes=[bc])
        bsr = alloc(es, "bsr", [4, 128])
        P.dma("sp", bsr[:], IN["od_gmlp_bs"], writes=[bc])
        wsT = alloc(es, "wsT", [128, 4, 128], BF16); bsT = alloc(es, "bsT", [128, 4]); bc2 = Buf()
        pt, pb = PSB[0]
        for g in range(4):
            TR(pt[:, g * 128:(g + 1) * 128], wsf[:, g, :], ident[:], [bc], [pb])
        for g in range(4):
            V("dve", "tensor_tensor", [pb, bc], [bc2], out=wsT[:, g, :], in0=pt[:, g * 128:(g + 1) * 128], in1=tril[:], op=ALU.mult)
        pt2, pb2 = PSB[1]
        TR(pt2[:, 0:4], bsr[0:4, :], ident[0:4, 0:4], [bc], [pb2])
        V("dve", "tensor_copy", [pb2], [bc2], out=bsT[:], in_=pt2[:, 0:4])
        ggb = alloc(es, "ggb", [128, 512]); gbb = alloc(es, "gbb", [128, 512])
        P.dma("sp", ggb[:], IN["od_gmlp_ln_g"].rearrange("(o n) -> o n", o=1).broadcast_to([128, 512]), writes=[bc2])
        P.dma("sp", gbb[:], IN["od_gmlp_ln_b"].rearrange("(o n) -> o n", o=1).broadcast_to([128, 512]), writes=[bc2])
        xs_r = Rot([(alloc(es, f"xs{i}", [128, 4, D]), Buf()) for i in range(2)])
        xT_r = Rot([(alloc(es, f"xT{i}", [128, 8, 512], BF16), Buf()) for i in range(2)])
        stg_r = Rot([(alloc(es, f"stg{i}", [128, 512], BF16), Buf()) for i in range(6)])
        va_r = []
        for i in range(3):
            t = alloc(es, f"va{i}", [128, 2, 130], BF16)
            b = Buf()
            V("pool", "memset", [], [b], ap=t[:], constant=1.0)
            va_r.append((t, b))
        va_r = Rot(va_r)
        gt_r = Rot([(alloc(es, f"gt{i}", [128, 24]), Buf()) for i in range(3)])
        gu_r = Rot([(alloc(es, f"gu{i}", [128, 512]), Buf()) for i in range(2)])
        gv_r = Rot([(alloc(es, f"gv{i}", [128, 512]), Buf()) for i in range(2)])
        vn_r = Rot([(alloc(es, f"vn{i}", [128, 512], BF16), Buf()) for i in range(2)])
        om_r = Rot([(alloc(es, f"om{i}", [128, 512]), Buf()) for i in range(2)])
        omT_r = Rot([(alloc(es, f"omT{i}", [128, 4, 512], BF16), Buf()) for i in range(2)])
        sm_r = Rot([small_ln(es, f"g{i}") for i in range(2)])
        psT = Rot(PSB[0:2])
        psM = Rot(PSB[2:8])
        tog = 0
        for tc in range(ntok // 512):
            sq = tc // 4
            t0 = (tc % 4) * 512
            xs, bxs = xs_r.next()
            xT, bxT = xT_r.next()
            load_xT(xsrc, tc * 512, 4, xs, bxs, xT, bxT, psT)
            fm = [("q1T", hp, hp * 128, 0.125) for hp in range(4)] + [("kcT", None, 512, None), ("vcT", None, 640, None),
                                                                       ("ksT", None, 768, None), ("kwT", None, 1024, None)]
            for dstn, hp, col0, scale in fm:
                pt, pb = psM.next()
                for kc in range(8):
                    MM(pt[:, :], W[:, kc, col0:col0 + 128], xT[:, kc, :], kc == 0, kc == 7, [bWq if hp is not None else bWkv, bxT], [pb])
                sg, bsg = stg_r.next()
                evac("act" if tog else "dve", sg[:], pt[:, :], [pb], [bsg], scale=scale)
                tog ^= 1
                if hp is None:
                    P.dma("sp", SC[dstn].ap[sq, :, t0:t0 + 512], sg[:], reads=[bsg], writes=[SC[dstn].b((sq, tc))])
                else:
                    P.dma("sp", SC[dstn].ap[sq, hp, :, t0:t0 + 512], sg[:], reads=[bsg], writes=[SC[dstn].b((sq, hp, tc))])
            omT, bomT = omT_r.next()
            for j in range(4):
                r0 = tc * 512 + j * 128
                xTj = lambda kc: xT[:, kc, j * 128:(j + 1) * 128]
                pt, pb = psM.next()
                for (c0, n, o0) in ((896, 128, 0), (1152, 128, 128), (1280, 24, 256)):
                    for kc in range(8):
                        MM(pt[:, o0:o0 + n], xT[:, kc, j * 128:(j + 1) * 128], W[:, kc, c0:c0 + n], kc == 0, kc == 7, [bWkv, bxT], [pb])
                va, bva = va_r.next()
                V("dve", "tensor_copy", [pb], [bva], out=va[:, 0, :].rearrange("p (g d) -> p g d", g=2)[:, :, 0:64],
                  in_=pt[:, 0:128].rearrange("p (g d) -> p g d", g=2))
                V("dve", "tensor_copy", [pb], [bva], out=va[:, 1, :].rearrange("p (g d) -> p g d", g=2)[:, :, 0:64],
                  in_=pt[:, 128:256].rearrange("p (g d) -> p g d", g=2))
                gt, bgt = gt_r.next()
                ACT(gt[:], pt[:, 256:280], AF.Sigmoid, [pb], [bgt])
                P.dma("sp", SC["vsA"].ap[r0:r0 + 128, :], va[:, 0, :], reads=[bva], writes=[SC["vsA"].b((sq, r0))])
                P.dma("sp", SC["vwA"].ap[r0:r0 + 128, :], va[:, 1, :], reads=[bva], writes=[SC["vwA"].b((sq, r0))])
                P.dma("sp", SC["gates"].ap[r0:r0 + 128, :], gt[:], reads=[bgt], writes=[SC["gates"].b((sq, r0))])
                pu, pbu = psM.next()
                pv, pbv = psM.next()
                for kc in range(8):
                    MM(pu[:, :], xT[:, kc, j * 128:(j + 1) * 128], W[:, kc, 1304:1816], kc == 0, kc == 7, [bWu_, bxT], [pbu])
                for kc in range(8):
                    MM(pv[:, :], xT[:, kc, j * 128:(j + 1) * 128], W[:, kc, 1816:2328], kc == 0, kc == 7, [bWv_, bxT], [pbv])
                gu, bgu = gu_r.next()
                gv, bgv = gv_r.next()
                ACT(gu[:], pu[:, :], AF.Gelu_apprx_tanh, [pbu], [bgu])
                ACT(gv[:], pv[:, :], AF.Gelu_apprx_tanh, [pbv], [bgv])
                st, mv, sd, rstd, nmr, tmp, bsm = sm_r.next()
                V("dve", "bn_stats", [bgv], [bsm], out=st[:, 0:6], in_=gv[:])
                V("dve", "bn_aggr", [bsm], [bsm], out=mv[:], in_=st[:, 0:6])
                ACT(sd[:], mv[:, 1:2], AF.Sqrt, [bsm], [bsm], bias=LN_EPS)
                V("dve", "reciprocal", [bsm], [bsm], out=rstd[:], in_=sd[:])
                V("dve", "tensor_scalar", [bsm], [bsm], out=nmr[:], in0=mv[:, 0:1], scalar1=rstd[:, 0:1], scalar2=-1.0,
                  op0=ALU.mult, op1=ALU.mult)
                ACT(gv[:], gv[:], AF.Identity, [bgv, bsm], [bgv], bias=nmr[:, 0:1], scale=rstd[:, 0:1])
                V("dve", "tensor_tensor", [bgv, bc2], [bgv], out=gv[:], in0=gv[:], in1=ggb[:], op=ALU.mult)
                vn, bvn = vn_r.next()
                V("dve", "tensor_tensor", [bgv, bc2], [bvn], out=vn[:], in0=gv[:], in1=gbb[:], op=ALU.add)
                pm, pbm = psM.next()
                for g in range(4):
                    MM(pm[:, g * 128:(g + 1) * 128], wsT[:, g, :], vn[:, g * 128:(g + 1) * 128], True, True, [bc2, bvn], [pbm])
                om, bom = om_r.next()
                for g in range(4):
                    V("dve", "scalar_tensor_tensor", [pbm, bgu, bc2], [bom], out=om[:, g * 128:(g + 1) * 128], in0=pm[:, g * 128:(g + 1) * 128],
                      scalar=bsT[:, g:g + 1], in1=gu[:, g * 128:(g + 1) * 128], op0=ALU.add, op1=ALU.mult)
                pt, pb = psM.next()
                for cc in range(4):
                    TR(pt[:, cc * 128:(cc + 1) * 128], om[:, cc * 128:(cc + 1) * 128], ident[:], [bom], [pb])
                ACT(omT[:, :, j * 128:(j + 1) * 128], pt[:, :].rearrange("p (c t) -> p c t", c=4), AF.Identity, [pb], [bomT])
            P.dma("sp", SC["omlpT"].ap[sq, :, :, t0:t0 + 512].rearrange("c p t -> p c t"), omT[:], reads=[bomT],
                  writes=[SC["omlpT"].b((sq, tc))])
        P.barrier()
        es.close()

    def pro_nsa():
        es = ExitStack()
        bc = Buf()
        trin = alloc(es, "trin", [128, 2, 128], BF16)
        cmn = alloc(es, "cmn", [128, S], BF16); ematn = alloc(es, "ematn", [128, 16, 128], BF16)
        sadd = alloc(es, "sadd", [128, 16, 32])
        P.dma("pool", trin[:], IN["trineg"], writes=[bc])
        P.dma("pool", cmn[:], IN["cmneg"], writes=[bc])
        P.dma("pool", ematn[:], IN["ematn"], writes=[bc])
        P.dma("sp", sadd[:], IN["scoreadd"], writes=[bc])
        W1 = {}
        W2 = {}
        cb = {}
        for kv, (w1n, w2n, posn) in (("k", ("od_cmpk_w1", "od_cmpk_w2", "od_cmpk_pos")), ("v", ("od_cmpv_w1", "od_cmpv_w2", "od_cmpv_pos"))):
            w1 = alloc(es, f"w1{kv}", [128, 32, 128], BF16)
            w2 = alloc(es, f"w2{kv}", [128, 128], BF16)
            bw = Buf()
            V("pool", "memset", [], [bw], ap=w1[:], constant=0.0)
            V("pool", "memset", [], [bw], ap=w2[:], constant=0.0)
            for g in range(2):
                P.dma("pool", w1[64 * g:64 * g + 64, :, 64 * g:64 * g + 64], IN[w1n].rearrange("(l d) e -> d l e", d=64), writes=[bw])
                P.dma("pool", w2[64 * g:64 * g + 64, 64 * g:64 * g + 64], IN[w2n], writes=[bw])
            posf = alloc(es, f"posf{kv}", [32, 128])
            for g in range(2):
                P.dma("sp", posf[:, 64 * g:64 * g + 64], IN[posn], writes=[bw])
            pt, pb = PSB[0]
            TR(pt[:, 0:32], posf[0:32, :], ident[0:32, 0:32], [bw], [pb])
            posT = alloc(es, f"posT{kv}", [128, 32], BF16)
            V("dve", "tensor_copy", [pb], [bw], out=posT[:], in_=pt[:, 0:32])
            pt2, pb2 = PSB[1]
            for l in range(32):
                MM(pt2[:, 0:1], w1[:, l, :], posT[:, l:l + 1], l == 0, l == 31, [bw], [pb2])
            cbias = alloc(es, f"cb{kv}", [128, 1])
            V("dve", "tensor_copy", [pb2], [bw], out=cbias[:], in_=pt2[:, 0:1])
            W1[kv] = (w1, bw)
            W2[kv] = w2
            cb[kv] = cbias
        raug = alloc(es, "raug", [128, 2, 97], BF16); braug = Buf()
        V("pool", "memset", [], [braug], ap=raug[:], constant=1.0)
        for g in range(2):
            P.dma("pool", raug[:, g, 65:97], IN["overlap"], writes=[braug])
        kcmpT = alloc(es, "kcmpT", [128, 128], BF16); bkc = Buf()
        vsA = alloc(es, "vsA", [128, 16, 2, 128], BF16); vwA = alloc(es, "vwA", [128, 16, 2, 128], BF16)
        bld = Buf()
        V("pool", "memset", [], [bld], ap=vsA[:], constant=0.0)
        V("pool", "memset", [], [bld], ap=vwA[:], constant=0.0)
        V("pool", "memset", [], [bkc], ap=kcmpT[:], constant=0.0)
        selT = alloc(es, "selT", [128, 2, S], BF16); bsel = [[Buf() for _ in range(4)] for _ in range(2)]
        V("pool", "memset", [], [bsel[g][t] for g in range(2) for t in range(4)], ap=selT[:], constant=0.0)
        return dict(es=es, bc=bc, trin=trin, cmn=cmn, ematn=ematn, sadd=sadd, W1=W1, W2=W2, cb=cb, raug=raug, braug=braug,
                    kcmpT=kcmpT, bkc=bkc, vsA=vsA, vwA=vwA, bld=bld, selT=selT, bsel=bsel)

    def stage_l1_nsa(pn):
        es = ExitStack()
        bc, trin, cmn, ematn, sadd, W1, W2, cb = pn["bc"], pn["trin"], pn["cmn"], pn["ematn"], pn["sadd"], pn["W1"], pn["W2"], pn["cb"]
        raug, braug, kcmpT, bkc, vsA, vwA, bld, selT, bsel = (pn["raug"], pn["braug"], pn["kcmpT"], pn["bkc"], pn["vsA"], pn["vwA"],
                                                              pn["bld"], pn["selT"], pn["bsel"])
        gel_r = Rot([(alloc(es, f"gel{i}", [128, 128], BF16), Buf()) for i in range(2)])
        q1 = alloc(es, "q1", [128, 8, S], BF16); kcT = alloc(es, "kcT", [128, S], BF16); vcT = alloc(es, "vcT", [128, S], BF16)
        ksT = alloc(es, "ksT", [128, S], BF16); kwT = alloc(es, "kwT", [128, S], BF16)
        gts = alloc(es, "gts", [128, 16, 24])
        V("pool", "memset", [], [bld], ap=q1[:, 0:4, :], constant=0.0)
        V("dve", "memset", [], [bld], ap=q1[:, 4:8, :], constant=0.0)
        O = alloc(es, "O", [128, 16, 512]); bO = [[Buf() for _ in range(8)] for _ in range(16)]
        imp = alloc(es, "imp", [128, 16, 2, 32]); bimp = [[Buf() for _ in range(2)] for _ in range(16)]
        eT_r = Rot([(alloc(es, f"eT{i}", [128, 512], BF16), Buf()) for i in range(3)])
        pT_r = Rot([(alloc(es, f"pT{i}", [128, 512], BF16), Buf()) for i in range(6)])
        oT_r = Rot([(alloc(es, f"oTs{i}", [65, 512]), Buf()) for i in range(3)])
        sm_r = Rot([(alloc(es, f"rd{i}", [128, 12]), Buf()) for i in range(8)])
        tm_r = Rot([(alloc(es, f"tmf{i}", [128, 4, 64]), Buf()) for i in range(3)])
        ti_r = Rot([(alloc(es, f"tif{i}", [128, 4, 32]), Buf()) for i in range(3)])
        sc_r = Rot([(alloc(es, f"sc{i}", [128, 32]), alloc(es, f"t8{i}", [128, 8]), alloc(es, f"sm{i}", [128, 32]), Buf()) for i in range(4)])
        onT_r = Rot([(alloc(es, f"onT{i}", [128, 4, 512], BF16), Buf()) for i in range(2)])
        TINY = 1e-30

        def finalize(acc4, bacc, hh, branch, tq, accumulate):
            j0 = 4 * tq
            rd, brd = sm_r.next()
            bos = [bO[j0 + j][hh] for j in range(4)]
            V("dve", "tensor_scalar", [bacc], [brd], out=rd[:, 0:4], in0=acc4[:, :, 64], scalar1=TINY, scalar2=None, op0=ALU.max)
            V("dve", "reciprocal", [brd], [brd], out=rd[:, 4:8], in_=rd[:, 0:4])
            V("dve", "tensor_tensor", [brd, bld], [brd], out=rd[:, 8:12], in0=rd[:, 4:8], in1=gts[:, j0:j0 + 4, 3 * hh + branch], op=ALU.mult)
            Oview = O[:, j0:j0 + 4, hh * 64:(hh + 1) * 64]
            gb = rd[:, 8:12].unsqueeze(2).broadcast_to([128, 4, 64])
            if accumulate:
                tm, btm = tm_r.next()
                V("dve", "tensor_tensor", [bacc, brd], [btm], out=tm[:], in0=acc4[:, :, 0:64], in1=gb, op=ALU.mult)
                V("pool", "tensor_tensor", [btm] + bos, bos, out=Oview, in0=Oview, in1=tm[:], op=ALU.add)
            else:
                V("dve", "tensor_tensor", [bacc, brd], bos, out=Oview, in0=acc4[:, :, 0:64], in1=gb, op=ALU.mult)
            if branch == 0:
                g = hh // 4
                bis = [bimp[j0 + j][g] for j in range(4)]
                rb = rd[:, 4:8].unsqueeze(2).broadcast_to([128, 4, 32])
                Iview = imp[:, j0:j0 + 4, g, :]
                if hh % 4 == 0:
                    V("dve", "tensor_tensor", [bacc, brd], bis, out=Iview, in0=acc4[:, :, 65:97], in1=rb, op=ALU.mult)
                else:
                    ti, bti = ti_r.next()
                    V("dve", "tensor_tensor", [bacc, brd], [bti], out=ti[:], in0=acc4[:, :, 65:97], in1=rb, op=ALU.mult)
                    V("pool", "tensor_tensor", [bti] + bis, bis, out=Iview, in0=Iview, in1=ti[:], op=ALU.add)

        def finalize_T(pacc, bpacc, hh, branch, tq, psT, tog):
            oTs, boTs = oT_r.next()
            evac("act" if tog else "dve", oTs[:], pacc[0:65, :], [bpacc], [boTs])
            pt, pb = psT
            for j in range(4):
                TR(pt[:, j * 65:(j + 1) * 65], oTs[0:65, j * 128:(j + 1) * 128], ident[0:65, 0:65], [boTs], [pb])
            finalize(pt[:, 0:260].rearrange("p (j c) -> p j c", c=65), pb, hh, branch, tq, True)

        for sq in range(nseq):
            for hh in range(8):
                g_, r_ = hh // 4, hh % 4
                P.dma("sp", q1[64 * g_:64 * g_ + 64, hh, :], SC["q1T"].ap[sq, r_, 64 * g_:64 * g_ + 64, :],
                      reads=SC["q1T"].all(lambda kk: kk[0] == sq and kk[1] == r_), writes=[bld])
            for nm, t in (("kcT", kcT), ("vcT", vcT), ("ksT", ksT), ("kwT", kwT)):
                P.dma("sp", t[:], SC[nm].ap[sq], reads=SC[nm].all(lambda kk: kk[0] == sq), writes=[bld])
            for nm, t in (("vsA", vsA), ("vwA", vwA)):
                for g_ in range(2):
                    P.dma("sp", t[:, :, g_, 0:65], SC[nm].ap[sq * S:(sq + 1) * S, 65 * g_:65 * g_ + 65].rearrange("(j p) d -> p j d", p=128),
                          reads=SC[nm].all(lambda kk: kk[0] == sq), writes=[bld])
            P.dma("sp", gts[:], SC["gates"].ap[sq * S:(sq + 1) * S, :].rearrange("(j p) d -> p j d", p=128),
                  reads=SC["gates"].all(lambda kk: kk[0] == sq), writes=[bld])
            for kv, src in (("k", kcT), ("v", vcT)):
                w1, bw = W1[kv]
                pt, pb = PSB[0] if kv == "k" else PSB[1]
                for l in range(32):
                    MM(pt[:, 0:127], w1[:, l, :], src[:, l:l + 2017:16], l == 0, l == 31, [bw, bld], [pb])
                gel, bgel = gel_r.next()
                ACT(gel[:, 0:127], pt[:, 0:127], AF.Gelu_apprx_tanh, [pb, bw], [bgel], bias=cb[kv][:, 0:1])
                pt2, pb2 = PSB[2] if kv == "k" else PSB[3]
                if kv == "k":
                    MM(pt2[:, 0:127], W2[kv][:], gel[:, 0:127], True, True, [bw, bgel], [pb2])
                    V("dve", "tensor_copy", [pb2, bkc], [bkc], out=kcmpT[:, 0:127], in_=pt2[:, 0:127])
                else:
                    MM(pt2[0:127, 0:128], gel[:, 0:127], W2[kv][:], True, True, [bw, bgel], [pb2])
                    V("dve", "tensor_copy", [pb2], [braug], out=raug[0:127, :, 0:64], in_=pt2[0:127, 0:128].rearrange("p (g d) -> p g d", g=2))
            psS = Rot(PSB[0:3])
            psA = Rot(PSB[3:6])
            for tq in range(4):
                t0 = tq * 512
                for hh in range(8):
                    g, r = hh // 4, hh % 4
                    pr = slice(64 * g, 64 * g + 64)
                    pt, pb = psS.next()
                    MM(pt[:, :], kcmpT[:, :], q1[:, hh, t0:t0 + 512], True, False, [bkc, bld], [pb])
                    MM(pt[:, :], identb[:], cmn[:, t0:t0 + 512], False, True, [bc, b_identb], [pb])
                    eT, beT = eT_r.next()
                    ACT(eT[:, :], pt[:, :], AF.Exp, [pb], [beT])
                    pa, pba = psA.next()
                    for j in range(4):
                        MM(pa[:, j * 97:(j + 1) * 97], eT[:, j * 128:(j + 1) * 128], raug[:, g, :], True, True, [beT, braug], [pba])
                    finalize(pa[:, 0:388].rearrange("p (j c) -> p j c", c=97), pba, hh, 0, tq, False)
                for g in range(2):
                    pt, pb = PSB[6 + g]
                    for j in range(4):
                        jj = 4 * tq + j
                        sc, t8, smk, bs_ = sc_r.next()
                        V("dve", "tensor_tensor", [bimp[jj][g], bc], [bs_], out=sc[:], in0=imp[:, jj, g, :], in1=sadd[:, jj, :], op=ALU.add)
                        V("dve", "max", [bs_], [bs_], out=t8[:], in_=sc[:])
                        V("dve", "tensor_scalar", [bs_], [bs_], out=smk[:], in0=sc[:], scalar1=t8[:, 7:8], scalar2=None, op0=ALU.is_lt)
                        TR(pt[0:32, j * 128:(j + 1) * 128], smk[:], ident[:], [bs_], [pb])
                    V("dve", "tensor_copy", [pb], [bsel[g][tq]], out=selT[0:32, g, t0:t0 + 512], in_=pt[0:32, :])
            psS = Rot(PSB[0:3])
            psT = PSB[7]
            tog = 0
            for tq in range(4):
                t0 = tq * 512
                for g in range(2):
                    pr = slice(64 * g, 64 * g + 64)
                    nkb = 4 * tq + 4
                    for kb in range(nkb):
                        rel = kb - 4 * tq
                        c0 = max(0, rel) * 128
                        for r in range(4):
                            pt, pb = psS.next()
                            MM(pt[:, c0:512], ksT[:, kb * 128:(kb + 1) * 128], q1[:, 4 * g + r, t0 + c0:t0 + 512], True, False, [bld], [pb])
                            MM(pt[:, c0:512], ematn[:, kb, :], selT[:, g, t0 + c0:t0 + 512], False, rel < 0, [bc, bsel[g][tq]], [pb])
                            if rel >= 0:
                                MM(pt[:, c0:c0 + 128], identb[:], trin[:, 0, :], False, True, [bc, b_identb], [pb])
                            pT, bpT = pT_r.next()
                            ACT(pT[:, c0:512], pt[:, c0:512], AF.Exp, [pb], [bpT])
                            pa, pba = PSB[3 + r]
                            MM(pa[:, c0:512], vsA[:, kb, g, :], pT[:, c0:512], kb == 0, kb == nkb - 1, [bpT, bld], [pba], sgc=True)
                    for r in range(4):
                        pa, pba = PSB[3 + r]
                        finalize_T(pa, pba, 4 * g + r, 1, tq, psT, tog)
                        tog ^= 1
            psS = Rot(PSB[0:3])
            psA = Rot(PSB[3:7])
            for tq in range(4):
                t0 = tq * 512
                for hh in range(8):
                    g, r = hh // 4, hh % 4
                    pr = slice(64 * g, 64 * g + 64)
                    pa, pba = psA.next()
                    kb_lo = max(0, 4 * tq - 4)
                    for kb in range(kb_lo, 4 * tq + 4):
                        rel = kb - (4 * tq - 4)
                        jlo = max(0, rel - 4)
                        jhi = min(3, rel)
                        c0, c1 = jlo * 128, (jhi + 1) * 128
                        pt, pb = psS.next()
                        MM(pt[:, c0:c1], kwT[:, kb * 128:(kb + 1) * 128], q1[:, hh, t0 + c0:t0 + c1], True, False, [bld], [pb])
                        if rel <= 3:
                            MM(pt[:, rel * 128:(rel + 1) * 128], identb[:], trin[:, 1, :], False, True, [bc, b_identb], [pb])
                        else:
                            MM(pt[:, (rel - 4) * 128:(rel - 3) * 128], identb[:], trin[:, 0, :], False, True, [bc, b_identb], [pb])
                        pT, bpT = pT_r.next()
                        ACT(pT[:, c0:c1], pt[:, c0:c1], AF.Exp, [pb], [bpT])
                        MM(pa[:, c0:c1], vwA[:, kb, g, :], pT[:, c0:c1], kb == kb_lo, kb == 4 * tq + 3, [bpT, bld], [pba], sgc=True)
                    finalize_T(pa, pba, hh, 2, tq, psT, tog)
                    tog ^= 1
            psTr = Rot(PSB[0:4])
            tog = 0
            for tq in range(4):
                onT, bon = onT_r.next()
                for j in range(4):
                    jj = 4 * tq + j
                    pt, pb = psTr.next()
                    for cc in range(4):
                        TR(pt[:, cc * 128:(cc + 1) * 128], O[:, jj, cc * 128:(cc + 1) * 128], ident[:], bO[jj][2 * cc:2 * cc + 2], [pb])
                    evac("act" if tog else "dve", onT[:, :, j * 128:(j + 1) * 128], pt[:, :].rearrange("p (c t) -> p c t", c=4), [pb], [bon])
                    tog ^= 1
                P.dma("sp", SC["onsaT"].ap[sq, :, :, tq * 512:(tq + 1) * 512].rearrange("c p t -> p c t"), onT[:], reads=[bon],
                      writes=[SC["onsaT"].b((sq, tq))])
        P.barrier()
        es.close()
        pn["es"].close()

    P.flush()
    if want("l0_inproj"):
        stage_l0_inproj()
    if want("l0_sb") and want("l0_conv"):
        stage_l0_sb(fuse_conv=True)
    else:
        if want("l0_sb"):
            stage_l0_sb()
        if want("l0_conv"):
            stage_l0_conv()
    xin = DT(IN["x"])
    pre0 = None
    if want("l0_out"):
        pre0 = stage_outproj(0, SC["osbT"], 4, 128, SC["ocvT"], "ev_w_out", xin, SC["x1"], "ln1_g", "ln1_b")
        if not want("l0_ffn"):
            pre0[0].close()
    if want("l0_ffn"):
        stage_ffn(0, SC["x1"], SC["x2"], pre0)
    pn = None
    if want("l1_nsa"):
        P.cur_bias = 1500
        pn = pro_nsa()
        P.cur_bias = 0
    if want("l1_inproj"):
        stage_l1_inproj(SC["x2"])
    if want("l1_nsa"):
        stage_l1_nsa(pn)
    pre1 = None
    if want("l1_out"):
        pre1 = stage_outproj(1, SC["onsaT"], 4, 128, SC["omlpT"], "od_w_out", SC["x2"], SC["x3"], "ln1_g", "ln1_b")
        if not want("l1_ffn"):
            pre1[0].close()
    if want("l1_ffn"):
        stage_ffn(1, SC["x3"], out_dt, pre1)
    outs = list(out_dt.all())
    for nm in debug_outs:
        outs += SC[nm].all()
    P.finish(outs)
    top.close()
    return nc, P


def prep_weights(inputs):
    w = {}
    for k in WEIGHT_SHAPES:
        a = np.asarray(inputs[k], dtype=np.float32)
        if k.startswith("ev_") or k.startswith("od_"):
            a = a[0]
        w[k] = np.ascontiguousarray(a)
    wi = w["od_w_in"].copy()
    q = wi[:, 0:512].reshape(D, 2, 4, 64)
    wi[:, 0:512] = q.transpose(0, 2, 1, 3).reshape(D, 512)
    w["od_w_in"] = wi
    return w


_CACHE = {}


def kernel(**inputs):
    x = np.asarray(inputs["x"], dtype=np.float32)
    B = x.shape[0]
    per = B // NCORES
    if "nc" not in _CACHE:
        _CACHE["nc"] = build(per * S)[0]
    nc = _CACHE["nc"]
    w = prep_weights(inputs)
    consts = host_consts()
    in_maps = []
    for c in range(NCORES):
        m = {"x": np.ascontiguousarray(x[c * per:(c + 1) * per].reshape(per * S, D))}
        m.update(w)
        for k, v in consts.items():
            m["c_" + k] = v
        in_maps.append(m)
    res = run_bass_kernel_spmd(nc, in_maps, core_ids=list(range(NCORES)))
    out = np.concatenate([np.asarray(r["out"]).reshape(per, S, D) for r in res.results], axis=0)
    return out.astype(np.float32)
```
